# Optimizing a Trainium2 kernel written in Bass

```python
import math
import jax, jax.numpy as jnp
from jax import lax
import numpy as np

D_MODEL = 1024
BATCH = 2
SEQ = 8192
DEPTH = 2

D_MIX = D_MODEL
HEAD_DIM = 64
N_HEADS = 8
N_KV_HEADS = 2
Q_PER_KV = N_HEADS // N_KV_HEADS
D_ATTN = N_HEADS * HEAD_DIM
D_KV = N_KV_HEADS * HEAD_DIM
WINDOW = 128
BLOCK = 128
N_GM_GROUPS = 8
GM_GROUP_DIM = 64
D_GM = N_GM_GROUPS * GM_GROUP_DIM
CHUNK = 128
D_IN = D_ATTN + 2 * D_KV + D_ATTN + 3 * D_GM
EPS = 1e-6
NEG_INF = -1e30

kernel_name = "hymba_style_bidir_attn_gmlp_hybrid"


def _rms(x, eps=EPS):
    xf = x.astype(jnp.float32)
    return (xf * lax.rsqrt(jnp.mean(xf * xf, axis=-1, keepdims=True) + eps)).astype(x.dtype)


def _alibi_slopes(n_heads):
    return 2.0 ** (-8.0 * jnp.arange(1, n_heads + 1, dtype=jnp.float32) / n_heads)


def _windowed_gqa(q, k, v, q_gain, k_gain, sink):
    b, s = q.shape[0], q.shape[1]
    nb = s // BLOCK
    q = _rms(q) * q_gain
    k = _rms(k) * k_gain
    qb = q.reshape(b, nb, BLOCK, N_KV_HEADS, Q_PER_KV, HEAD_DIM)
    pad = ((0, 0), (BLOCK, BLOCK), (0, 0), (0, 0))
    kp = jnp.pad(k, pad).reshape(b, nb + 2, BLOCK, N_KV_HEADS, HEAD_DIM)
    vp = jnp.pad(v, pad).reshape(b, nb + 2, BLOCK, N_KV_HEADS, HEAD_DIM)
    kb = jnp.concatenate([kp[:, :-2], kp[:, 1:-1], kp[:, 2:]], axis=2)
    vb = jnp.concatenate([vp[:, :-2], vp[:, 1:-1], vp[:, 2:]], axis=2)
    scores = jnp.einsum('bnqkgd,bnskd->bnkgqs', qb, kb).astype(jnp.float32) / math.sqrt(HEAD_DIM)
    blk = jnp.arange(nb)[:, None, None]
    qpos = blk * BLOCK + jnp.arange(BLOCK)[None, :, None]
    kpos = blk * BLOCK - BLOCK + jnp.arange(3 * BLOCK)[None, None, :]
    dist = jnp.abs(kpos - qpos).astype(jnp.float32)
    valid = (dist <= WINDOW) & (kpos >= 0) & (kpos < s)
    slopes = _alibi_slopes(N_HEADS).reshape(N_KV_HEADS, Q_PER_KV)
    scores = scores - slopes[None, None, :, :, None, None] * dist[None, :, None, None]
    scores = jnp.where(valid[None, :, None, None], scores, NEG_INF)
    sink_col = jnp.broadcast_to(
        sink.astype(jnp.float32).reshape(N_KV_HEADS, Q_PER_KV)[None, None, :, :, None, None],
        scores.shape[:-1] + (1,))
    probs = jax.nn.softmax(jnp.concatenate([scores, sink_col], axis=-1), axis=-1)[..., :-1]
    out = jnp.einsum('bnkgqs,bnskd->bnqkgd', probs.astype(v.dtype), vb)
    return out.reshape(b, s, D_ATTN)


def _chunked_gmlp(u, vg, w_s, b_s):
    b, s = u.shape[0], u.shape[1]
    nc = s // CHUNK
    vn = _rms(vg.reshape(b, nc, CHUNK, N_GM_GROUPS, GM_GROUP_DIM))
    sv = jnp.einsum('gts,bcsge->bctge', w_s, vn) + b_s.T[None, None, :, :, None]
    return u * sv.reshape(b, s, D_GM)


def setup_inputs(seed: int = 0) -> dict:
    key = jax.random.key(seed)
    ks = jax.random.split(key, 12)
    f32 = jnp.float32
    x = jax.random.normal(ks[0], (BATCH, SEQ, D_MODEL), f32)
    c = jax.random.normal(ks[1], (BATCH, D_MODEL), f32)
    w_ada = jax.random.normal(ks[2], (DEPTH, D_MODEL, 3 * D_MODEL), f32) * D_MODEL ** -0.5
    b_ada = jax.random.normal(ks[3], (DEPTH, 3 * D_MODEL), f32) * 0.02
    norm_gain = 1.0 + 0.01 * jax.random.normal(ks[4], (DEPTH, D_MODEL), f32)
    w_in = jax.random.normal(ks[5], (DEPTH, D_MODEL, D_IN), f32) * D_MODEL ** -0.5
    q_gain = 1.0 + 0.01 * jax.random.normal(ks[6], (DEPTH, HEAD_DIM), f32)
    k_gain = 1.0 + 0.01 * jax.random.normal(ks[7], (DEPTH, HEAD_DIM), f32)
    sink = jax.random.normal(ks[8], (DEPTH, N_HEADS), f32) * 0.5
    w_s = jax.random.normal(ks[9], (DEPTH, N_GM_GROUPS, CHUNK, CHUNK), f32) * (0.5 * CHUNK ** -0.5)
    b_s = 1.0 + 0.01 * jax.random.normal(ks[10], (DEPTH, N_GM_GROUPS, CHUNK), f32)
    w_out = jax.random.normal(ks[11], (DEPTH, D_MIX, D_MODEL), f32) * D_MIX ** -0.5
    return {"x": x, "c": c, "w_ada": w_ada, "b_ada": b_ada, "norm_gain": norm_gain,
            "w_in": w_in, "q_gain": q_gain, "k_gain": k_gain, "sink": sink,
            "w_s": w_s, "b_s": b_s, "w_out": w_out}


def reference(x, c, w_ada, b_ada, norm_gain, w_in, q_gain, k_gain, sink, w_s, b_s, w_out):
    b, s, _ = x.shape
    cond = jax.nn.silu(c)
    splits = np.cumsum([D_ATTN, D_KV, D_KV, D_ATTN, D_GM, D_GM])
    for l in range(DEPTH):
        ada = cond @ w_ada[l] + b_ada[l]
        shift, scale, gate = jnp.split(ada, 3, axis=-1)
        h = _rms(x) * norm_gain[l]
        h = h * (1.0 + scale[:, None, :]) + shift[:, None, :]
        proj = h @ w_in[l]
        q, k, v, g_attn, u, v_gm, g_gm = jnp.split(proj, splits, axis=-1)
        attn = _windowed_gqa(q.reshape(b, s, N_HEADS, HEAD_DIM),
                             k.reshape(b, s, N_KV_HEADS, HEAD_DIM),
                             v.reshape(b, s, N_KV_HEADS, HEAD_DIM),
                             q_gain[l], k_gain[l], sink[l])
        gm = _chunked_gmlp(u, v_gm, w_s[l], b_s[l])
        y = jnp.concatenate([attn * jax.nn.silu(g_attn), gm * jax.nn.silu(g_gm)], axis=-1)
        x = x + gate[:, None, :] * (y @ w_out[l])
    return x
```

```python
import contextlib
import numpy as np
import ml_dtypes
import concourse.bass as bass
import concourse.mybir as mybir
from concourse.bass_utils import run_bass_kernel_spmd

F32 = mybir.dt.float32
BF16 = mybir.dt.bfloat16
AF = mybir.ActivationFunctionType
ALU = mybir.AluOpType
AX = mybir.AxisListType

D = 1024
SEQ = 8192
NCORES = 8
DEPTH = 2
D_IN = 2816
NT = 2
OWN = 16 // NT
NB = OWN + 4
NXR = NB - 2
EPS = 1e-6
MASKV = -30000.0


class TT:
    __slots__ = ("name", "w", "r")

    def __init__(self, name):
        self.name = name
        self.w = None
        self.r = []


class Eng:
    def __init__(self, handle, name, sem):
        self.h = handle
        self.name = name
        self.sem = sem
        self.n = 0
        self.seen = {}


class DSem:
    def __init__(self, sem):
        self.sem = sem
        self.n = 0


def _bc_last(ap, n):
    return bass.AP(tensor=ap.tensor, offset=ap.offset, ap=[list(x) for x in ap.ap] + [[0, n]])


def _bc_col(col, n):
    return bass.AP(tensor=col.tensor, offset=col.offset, ap=[list(col.ap[0]), [0, n]])


def build_program():
    nc = bass.Bass("TRN2", target_bir_lowering=False)
    es = contextlib.ExitStack()

    def dram(name, shape, dt, kind):
        return nc.dram_tensor(name, list(shape), dt, kind=kind).ap()

    IN = "ExternalInput"
    xin = dram("xin", [NT, NB * 128, D], F32, IN)
    kmask_d = dram("kmask", [128, NT * NB], F32, IN)
    ccol_d = dram("ccol", [128, 8], F32, IN)
    w_ada = dram("w_ada", [DEPTH, D, 3 * D], F32, IN)
    bada_col_d = dram("bada_col", [DEPTH, 128, 16], F32, IN)
    bada_gate_d = dram("bada_gate", [DEPTH, 1, D], F32, IN)
    ng_col_d = dram("ng_col", [DEPTH, 128, 8], F32, IN)
    w_in = dram("w_in", [DEPTH, D, D_IN], F32, IN)
    qg_col_d = dram("qg_col", [128, DEPTH], F32, IN)
    kg_col_d = dram("kg_col", [128, DEPTH], F32, IN)
    sink_d = dram("sinkp", [DEPTH, 1, 8], F32, IN)
    wsT_d = dram("wsT", [DEPTH, 128, 8, 128], F32, IN)
    bs_d = dram("bs", [DEPTH, 8, 128], F32, IN)
    w_out = dram("w_out", [DEPTH, D, D], F32, IN)
    ident_d = dram("ident", [128, 128], F32, IN)
    bd_d = dram("bd", [128, 128], BF16, IN)
    identb_d = dram("identb", [128, 128], BF16, IN)
    lb_d = dram("lb", [128, 128], BF16, IN)
    e_d = dram("econst", [128, 6, 512], BF16, IN)
    out_d = dram("out", [NT, OWN * 128, D], F32, "ExternalOutput")

    def sb(name, shape, dt):
        return es.enter_context(nc.sbuf_tensor("sb_" + name, list(shape), dt))

    def sem(name):
        return es.enter_context(nc.semaphore(name))

    PE = Eng(nc.tensor, "pe", sem("s_pe"))
    ACT = Eng(nc.scalar, "act", sem("s_act"))
    DVE = Eng(nc.vector, "dve", sem("s_dve"))
    POOL = Eng(nc.gpsimd, "pool", sem("s_pool"))
    SP = Eng(nc.sync, "sp", sem("s_sp"))

    def _waits(eng, reads, writes, preads=()):
        need = {}
        deps = []
        for t in reads:
            if t.w is not None:
                deps.append(t.w)
        for t in preads:
            if t.w is not None:
                deps.append(t.w)
            for m in t.r:
                if m[2] != eng.name:
                    deps.append(m)
        for t in writes:
            if t.w is not None:
                deps.append(t.w)
            deps.extend(t.r)
        for (s, val, en) in deps:
            if en == eng.name and eng.name == "pe":
                continue
            if eng.seen.get(s, 0) >= val:
                continue
            if need.get(s, (None, 0))[1] < val:
                need[s] = (s, val)
        for s, val in need.values():
            eng.h.wait_ge(s, val)
            eng.seen[s] = val

    def emit(eng, fn, reads=(), writes=(), signal=True, preads=()):
        _waits(eng, reads, writes, preads)
        inst = fn()
        if signal:
            eng.n += 1
            inst.then_inc(eng.sem, 1)
            mark = (eng.sem, eng.n, eng.name)
        else:
            mark = (eng.sem, eng.n + 1, eng.name)
        for t in reads:
            t.r.append(mark)
        for t in preads:
            t.r.append(mark)
        for t in writes:
            t.w = mark
            t.r = []
        return inst

    def dma(q, ds, out, in_, reads=(), writes=(), nodep=False):
        if not nodep:
            _waits(q, reads, writes)
        inst = q.h.dma_start(out=out, in_=in_)
        ds.n += 16
        inst.then_inc(ds.sem, 16)
        mark = (ds.sem, ds.n, "dma")
        for t in reads:
            t.r.append(mark)
        for t in writes:
            t.w = mark
            t.r = []

    def pe(fn, reads, writes, signal=True):
        return emit(PE, fn, reads, writes, signal)

    def act(fn, reads, writes, preads=()):
        return emit(ACT, fn, reads, writes, True, preads)

    def dve(fn, reads, writes, preads=()):
        return emit(DVE, fn, reads, writes, True, preads)

    def pool(fn, reads, writes):
        return emit(POOL, fn, reads, writes)

    xres = [sb(f"xres{i}", [128, D], F32) for i in range(NXR)]
    xres_tt = [[TT(f"xres{i}_{h}") for h in range(2)] for i in range(NXR)]
    xres_ds = [DSem(sem(f"d_x{i}")) for i in range(NXR)]
    xr = [sb(f"xr{i}", [128, D], BF16) for i in range(2)]
    xr_tt = [TT(f"xr{i}") for i in range(2)]
    xh = sb("xh", [128, D], F32)
    xh_tt = TT("xh")
    xh_ds = DSem(sem("d_xh"))
    hT = [sb(f"hT{i}", [128, 8, 256], BF16) for i in range(2)]
    hT_tt = [[TT(f"hT{i}_{c}") for c in range(8)] for i in range(2)]
    Win = sb("Win", [128, 8, 2560], BF16)
    SEC = {"q": (0, 0), "ga": (512, 768), "u": (1024, 1280), "vgm": (1536, 1792), "gg": (2048, 2304)}
    Wsec_tt = {k: TT("Win_" + k) for k in SEC}
    Wsec_ds = {k: DSem(sem("d_w" + k)) for k in SEC}
    Wkv = sb("Wkv", [128, 8, 256], BF16)
    Wkv_tt = TT("Wkv")
    Wkv_ds = DSem(sem("d_wkv"))
    Wout = sb("Wout", [128, 8, D], BF16)
    Wout_tt = [TT(f"Wout{h}") for h in range(2)]
    Wout_ds = [DSem(sem(f"d_wo{h}")) for h in range(2)]
    Kpad = [[sb(f"Kpad{h}{kv}", [128, NB * 128], BF16) for kv in range(2)] for h in range(2)]
    Kpad_tt = [[[TT(f"Kpad{h}{kv}_{b}") for b in range(NB)] for kv in range(2)] for h in range(2)]
    Vaug = sb("Vaug", [128, NB, 2, 128], BF16)
    Vaug_tt = [TT(f"Vaug{b}") for b in range(NB)]
    Ec = sb("Ec", [128, 6, 512], BF16)
    ident = sb("ident", [128, 128], F32)
    BD = sb("BD", [128, 128], BF16)
    identb = sb("identb", [128, 128], BF16)
    Lb = sb("Lb", [128, 128], BF16)
    WsT = sb("WsT", [128, DEPTH, 8, 128], BF16)
    Rb = sb("Rb", [128, DEPTH, 4, 128], BF16)
    gate_bc = sb("gate_bc", [128, DEPTH, D], F32)
    gate_tt = [[TT(f"gate{l}_{i}") for i in range(4)] for l in range(DEPTH)]
    gate_ds = [DSem(sem(f"d_gate{l}")) for l in range(DEPTH)]
    wst = [sb(f"wst{i}", [128, 8, 256], BF16) for i in range(2)]
    wst_tt = [TT(f"wst{i}") for i in range(2)]
    wst_ds = [DSem(sem(f"d_wst{i}")) for i in range(2)]
    condbc = sb("condbc", [128, 8, 128], BF16)
    cond_bf = sb("cond_bf", [128, 8], BF16)
    ccol = sb("ccol", [128, 8], F32)
    kmask = sb("kmask", [128, NT * NB], F32)
    bada_col = sb("bada_col", [128, DEPTH, 16], F32)
    ng_col = sb("ng_col", [128, DEPTH, 8], F32)
    adacol = sb("adacol", [128, DEPTH, 16], F32)
    Gc = sb("Gc", [128, DEPTH, 8], F32)
    ada_tt = [TT("ada0"), TT("ada1")]
    qg_col = sb("qg_col", [128, DEPTH], F32)
    kg_col = sb("kg_col", [128, DEPTH], F32)
    kg8 = sb("kg8", [128, DEPTH], F32)
    es_sl = sb("es_sl", [128, DEPTH, 8], F32)
    mhalf = sb("mhalf", [128, 1], F32)
    eps_col = sb("eps_col", [128, 1], F32)
    const_tt = TT("consts")
    const_ds = DSem(sem("d_const"))
    ssb = [sb(f"ssb{i}", [128, 2], F32) for i in range(2)]
    ssb_tt = [[TT(f"ssb{i}_{k}") for k in range(2)] for i in range(2)]
    msb = [sb(f"msb{i}", [128, 2], F32) for i in range(2)]
    msb_tt = [TT(f"msb{i}") for i in range(2)]
    ksq = sb("ksq", [128, 256], BF16)
    ksq_tt = TT("ksq")
    tk = sb("tk", [128, 256], F32)
    tk_tt = TT("tk")
    qsq = [sb(f"qsq{i}", [128, 512], BF16) for i in range(2)]
    qsq_tt = [TT(f"qsq{i}") for i in range(2)]
    tq = [sb(f"tq{i}", [128, 512], F32) for i in range(2)]
    tq_tt = [TT(f"tq{i}") for i in range(2)]
    QnT = sb("QnT", [128, 2, 4, 128], BF16)
    qraw = [sb(f"qraw{i}", [128, 512], F32) for i in range(2)]
    qraw_tt = [TT(f"qraw{i}") for i in range(2)]
    QnT_tt = [TT(f"QnT{j}") for j in range(4)]
    sg = sb("sg", [128, 4, 256], BF16)
    sg_tt = [TT(f"sg{j}") for j in range(2)]
    sgg = sb("sgg", [128, 4, 256], BF16)
    sgg_tt = [TT(f"sgg{j}") for j in range(2)]
    ug = sb("ug", [128, 4, 256], BF16)
    ug_tt = [TT(f"ug{j}") for j in range(2)]
    vsq = [sb(f"vsq{i}", [128, 512], BF16) for i in range(2)]
    vsq_tt = [TT(f"vsq{i}") for i in range(2)]
    ssv = [sb(f"ssv{i}", [128, 8], F32) for i in range(2)]
    ssv_tt = [TT(f"ssv{i}") for i in range(2)]
    vnpad = [sb(f"vnpad{i}", [128, 8, 128], BF16) for i in range(2)]
    vnpad_tt = [TT(f"vnpad{i}") for i in range(2)]
    PT = [sb(f"PT{i}", [128, 512], BF16) for i in range(6)]
    PT_tt = [TT(f"PT{i}") for i in range(6)]
    rden = [sb(f"rden{i}", [128, 512], F32) for i in range(2)]
    rden_tt = [TT(f"rden{i}") for i in range(2)]
    ya = sb("ya", [128, 4, 128], F32)
    ya_tt = [[TT(f"ya{kv}{h}") for h in range(2)] for kv in range(2)]
    yT = [sb(f"yT{i}", [128, 8, 128], BF16) for i in range(2)]
    yTA_tt = [TT(f"yTA{i}") for i in range(2)]
    yTM_tt = [TT(f"yTM{i}") for i in range(2)]

    psum = [es.enter_context(nc.psum_tensor(f"ps{i}", [128, 512], F32)) for i in range(8)]
    R_ap = [psum[i][:, :] for i in range(8)]
    R_tt = [TT(f"psR{i}") for i in range(8)]
    rot = [0]

    def alloc_R():
        i = rot[0] % 8
        rot[0] += 1
        return R_ap[i], R_tt[i]

    cond_tt, kc_tt, adac_tt, idb_tt, sm_tt = TT("cond"), TT("kc"), TT("adac"), TT("idb"), TT("sm")
    cond_ds, adac_ds, idb_ds, sm_ds = (DSem(sem("d_cond")), DSem(sem("d_adac")), DSem(sem("d_idb")),
                                       DSem(sem("d_sm")))
    dma(SP, cond_ds, ccol[:], ccol_d, writes=[cond_tt])
    dma(SP, xh_ds, xh[:], xin[0, 0:128, :], writes=[xh_tt])
    pre_issued = {(0, 0)}
    for b in range(1, 5):
        dma(SP, xres_ds[b - 1], xres[b - 1][:], xin[0, b * 128:(b + 1) * 128, :], writes=xres_tt[b - 1])
    dma(SP, idb_ds, identb[:], identb_d, writes=[idb_tt], nodep=True)
    dma(SP, idb_ds, BD[:], bd_d, writes=[idb_tt], nodep=True)
    dma(SP, adac_ds, bada_col[:], bada_col_d.rearrange("l p c -> p l c"), writes=[adac_tt], nodep=True)
    dma(SP, adac_ds, ng_col[:], ng_col_d.rearrange("l p c -> p l c"), writes=[adac_tt], nodep=True)
    dma(SP, sm_ds, kg_col[:], kg_col_d, writes=[sm_tt], nodep=True)
    dma(SP, sm_ds, qg_col[:], qg_col_d, writes=[sm_tt], nodep=True)
    dma(SP, sm_ds, kmask[:], kmask_d, writes=[sm_tt], nodep=True)

    def cload(dst, src):
        dma(SP, const_ds, dst, src, writes=[const_tt], nodep=True)

    cload(Ec[:], e_d)
    cload(Lb[:], lb_d)
    for l in range(DEPTH):
        cload(es_sl[:, l, :], sink_d[l].partition_broadcast(128))
    dve(lambda: nc.vector.memset(mhalf[:], -0.5), [], [kc_tt])
    dve(lambda: nc.vector.memset(eps_col[:], EPS), [], [kc_tt])
    for h in range(2):
        for kv in range(2):
            pool(lambda h=h, kv=kv: nc.gpsimd.memset(Kpad[h][kv][:], 0.0), [], [x for x in Kpad_tt[h][kv]])
    pool(lambda: nc.gpsimd.memset(Vaug[:], 1.0), [], Vaug_tt)
    for i in range(2):
        pool(lambda i=i: nc.gpsimd.memset(vnpad[i][:], 0.0), [], [vnpad_tt[i]])
    wsT_ds = DSem(sem("d_wsT"))
    wsT_tt = TT("wsT")
    rb_tt = TT("rb")
    bs_ds = DSem(sem("d_bs"))
    act(lambda: nc.scalar.activation(out=cond_bf[:], in_=ccol[:], func=AF.Silu), [cond_tt], [cond_tt])
    dve(lambda: nc.vector.tensor_copy(out=condbc[:], in_=_bc_last(cond_bf[:], 128)), [cond_tt], [cond_tt])
    dve(lambda: nc.vector.tensor_scalar(out=kg8[:], in0=kg_col[:], scalar1=0.125, scalar2=None,
                                        op0=ALU.mult), [sm_tt], [sm_tt])

    stg_l = [gate_bc[:, 0, 512 * l:512 * (l + 1)].rearrange("p (a b) -> p a b", a=4) for l in range(DEPTH)]
    stg2 = gate_bc[:, 1, 0:512].rearrange("p (a b) -> p a b", a=4)
    stg_tts = [rb_tt] + gate_tt[0] + gate_tt[1]
    stgz_tt = TT("stgz")
    bsrow_tt = []
    dve(lambda: nc.vector.memset(gate_bc[:, 0, :], 0.0), [], stg_tts + [stgz_tt])
    for l in range(DEPTH):
        bsv = bs_d[l].rearrange("(pr two) t -> two pr t", two=2)
        for (prt, two) in ((0, 0), (32, 1), (64, 0), (96, 1)):
            tt_ = TT(f"bsrow{l}_{prt}")
            bsrow_tt.append(tt_)
            dma(SP, bs_ds, stg_l[l][prt:prt + 1, :, :], bsv[two:two + 1], reads=[stgz_tt], writes=[tt_])

    def late_setup():
        act(lambda: nc.scalar.activation(out=es_sl[:], in_=es_sl[:], func=AF.Exp), [const_tt], [const_tt])
        for l in range(DEPTH):
            dve(lambda l=l: nc.vector.tensor_copy(out=Rb[:, l], in_=stg_l[l]), stg_tts + bsrow_tt, stg_tts)
            dve(lambda l=l: nc.vector.tensor_tensor(out=stg2[64:128], in0=stg_l[l][64:128], in1=Rb[64:128, l],
                                                    op=ALU.subtract), stg_tts, stg_tts)
            dve(lambda l=l: nc.vector.tensor_copy(out=Rb[64:128, l], in_=stg2[64:128]), stg_tts, stg_tts)
        for l in range(DEPTH):
            dma(SP, gate_ds[l], gate_bc[:, l, :], bada_gate_d[l].partition_broadcast(128),
                writes=[rb_tt] + gate_tt[l])

    def load_wkv(l):
        dma(POOL, Wkv_ds, Wkv[:], w_in[l][:, 512:768].rearrange("(kc p) n -> p kc n", p=128),
            writes=[Wkv_tt])

    def load_wsec(l, k):
        c0, s0 = SEC[k]
        dma(POOL, Wsec_ds[k], Win[:, :, c0:c0 + 512],
            w_in[l][:, s0:s0 + 512].rearrange("(kc p) n -> p kc n", p=128), writes=[Wsec_tt[k]])

    def load_wout(l):
        for h in range(2):
            dma(POOL, Wout_ds[h], Wout[:, 4 * h:4 * h + 4, :],
                w_out[l][512 * h:512 * h + 512, :].rearrange("(kc p) n -> p kc n", p=128), writes=[Wout_tt[h]])

    def ada_tasks(l, bufs=None):
        def pbuf(pc):
            if bufs is not None and pc in bufs:
                return bufs[pc]
            k = pc % 2
            return wst[k], [wst_tt[k]], wst_ds[k]

        def piece_dma(pc):
            w_, tts_, ds_ = pbuf(pc)
            dma(POOL, ds_, w_[:], w_ada[l][:, pc * 256:(pc + 1) * 256].rearrange("(kc p) n -> p kc n", p=128),
                writes=tts_)

        def task(pc):
            if bufs is None:
                if pc + 1 < 12:
                    piece_dma(pc + 1)
            else:
                for q in {2: (8, 9)}.get(pc, ()):
                    piece_dma(q)
            wk, wk_tts, _ = pbuf(pc)
            if pc < 8:
                pada, pada_tt = alloc_R()
                for dc in range(2):
                    for kc in range(8):
                        pe(lambda kc=kc, dc=dc: nc.tensor.matmul(
                            pada[:, dc:dc + 1], lhsT=wk[:, kc, dc * 128:(dc + 1) * 128],
                            rhs=cond_bf[:, kc:kc + 1], start=(kc == 0), stop=(kc == 7)),
                           wk_tts + [cond_tt], [pada_tt], signal=(kc == 7 and dc == 1))
                dve(lambda: nc.vector.tensor_tensor(out=adacol[:, l, 2 * pc:2 * pc + 2], in0=pada[:, 0:2],
                                                    in1=bada_col[:, l, 2 * pc:2 * pc + 2], op=ALU.add),
                    [adac_tt], [ada_tt[l]], preads=[pada_tt])
                if pc == 7:
                    dve(lambda: nc.vector.scalar_tensor_tensor(out=Gc[:, l], in0=adacol[:, l, 8:16], scalar=1.0,
                                                               in1=ng_col[:, l], op0=ALU.add, op1=ALU.mult),
                        [ada_tt[l], adac_tt], [ada_tt[l]])
            else:
                g = pc - 8
                pg, pg_tt = alloc_R()
                for kc in range(8):
                    pe(lambda kc=kc: nc.tensor.matmul(pg[:, 0:256], lhsT=condbc[:, kc, :], rhs=wk[:, kc, :],
                                                      start=(kc == 0), stop=(kc == 7)),
                       wk_tts + [cond_tt], [pg_tt], signal=(kc == 7))
                dve(lambda: nc.vector.tensor_tensor(out=gate_bc[:, l, g * 256:(g + 1) * 256], in0=pg[:, 0:256],
                                                    in1=gate_bc[:, l, g * 256:(g + 1) * 256], op=ALU.add),
                    [gate_tt[l][g]], [gate_tt[l][g]], preads=[pg_tt])

            if bufs is not None and pc == 7:
                piece_dma(10)
                piece_dma(11)

        def first():
            for q in (range(8) if bufs is not None else range(1)):
                piece_dma(q)

        return [first] + [(lambda pc=pc: task(pc)) for pc in range(12)]

    unit_ctr = [0]

    def p1a(t, l, blocks, halo_dma):
        n = len(blocks)
        u = unit_ctr[0] % 2
        unit_ctr[0] += 1
        srcs = []
        for i, b in enumerate(blocks):
            if halo_dma:
                if (t, b) in pre_issued:
                    pre_issued.discard((t, b))
                else:
                    dma(SP, xh_ds, xh[:], xin[t, b * 128:(b + 1) * 128, :], writes=[xh_tt])
                srcs.append((xh[:], [xh_tt]))
            else:
                srcs.append((xres[b - 1][:], xres_tt[b - 1]))
        for i in range(n):
            sap, stt = srcs[i]
            jk = i
            act(lambda i=i, sap=sap, jk=jk: nc.scalar.activation(out=xr[jk][:], in_=sap, func=AF.Square,
                                                                 accum_out=ssb[u][:, i:i + 1]),
                stt, [ssb_tt[u][i], xr_tt[jk]])
        act(lambda: nc.scalar.activation(out=msb[u][:, 0:n], in_=ssb[u][:, 0:n], func=AF.Ln, bias=eps_col[:, 0:1],
                                         scale=1.0 / D),
            ssb_tt[u][0:n] + [kc_tt], [msb_tt[u]])
        act(lambda: nc.scalar.activation(out=msb[u][:, 0:n], in_=msb[u][:, 0:n], func=AF.Exp, scale=-0.5),
            [msb_tt[u]], [msb_tt[u]])
        for i in range(n):
            sap, stt = srcs[i]
            dve(lambda i=i, sap=sap: nc.vector.tensor_scalar(out=xr[i][:], in0=sap, scalar1=msb[u][:, i:i + 1],
                                                             scalar2=None, op0=ALU.mult),
                list(stt) + [msb_tt[u]], [xr_tt[i]])

    def p1b(t, l, blocks, hb):
        n = len(blocks)
        H = hT[hb]
        Htt = hT_tt[hb]
        for c in range(8):
            pa, pa_tt = alloc_R()
            pab = pa.bitcast(BF16)
            for i in range(n):
                pe(lambda i=i, c=c: nc.tensor.transpose(pab[:, i * 128:(i + 1) * 128],
                                                        xr[i][:, c * 128:(c + 1) * 128], identb[:]),
                   [xr_tt[i], idb_tt], [pa_tt], signal=(i == n - 1))
            if c % 4 == 0:
                act(lambda c=c, pab=pab: nc.scalar.activation(out=H[:, c, 0:n * 128], in_=pab[:, 0:n * 128], func=AF.Identity,
                                                     bias=adacol[:, l, c:c + 1], scale=Gc[:, l, c:c + 1]),
                    [ada_tt[l]], [Htt[c]], preads=[pa_tt])
            else:
                dve(lambda c=c, pab=pab: nc.vector.tensor_scalar(out=H[:, c, 0:n * 128], in0=pab[:, 0:n * 128],
                                                        scalar1=Gc[:, l, c:c + 1], scalar2=adacol[:, l, c:c + 1],
                                                        op0=ALU.mult, op1=ALU.add),
                    [ada_tt[l]], [Htt[c]], preads=[pa_tt])
        pk, pk_tt = alloc_R()
        for kc in range(8):
            pe(lambda kc=kc: nc.tensor.matmul(pk[:, 0:n * 128], lhsT=Wkv[:, kc, 0:128], rhs=H[:, kc, 0:n * 128],
                                              start=(kc == 0), stop=(kc == 7)),
               [Wkv_tt, Htt[kc]], [pk_tt], signal=(kc == 7))
        act(lambda: nc.scalar.activation(out=ksq[:, 0:n * 128], in_=pk[:, 0:n * 128], func=AF.Square),
            [], [ksq_tt], preads=[pk_tt])
        pss, pss_tt = alloc_R()
        pe(lambda: nc.tensor.matmul(pss[:, 0:n * 128], lhsT=BD[:], rhs=ksq[:, 0:n * 128], start=True, stop=True),
           [ksq_tt, idb_tt], [pss_tt])
        act(lambda: nc.scalar.activation(out=tk[:, 0:n * 128], in_=pss[:, 0:n * 128], func=AF.Ln,
                                         bias=eps_col[:, 0:1], scale=1.0 / 64),
            [kc_tt], [tk_tt], preads=[pss_tt])
        act(lambda: nc.scalar.activation(out=tk[:, 0:n * 128], in_=tk[:, 0:n * 128], func=AF.Exp, scale=-0.5),
            [tk_tt], [tk_tt])
        b0 = blocks[0]
        for kv in range(2):
            for h in range(2):
                dve(lambda kv=kv, h=h: nc.vector.scalar_tensor_tensor(
                    out=Kpad[h][kv][h * 64:(h + 1) * 64, b0 * 128:(b0 + n) * 128],
                    in0=pk[kv * 64:(kv + 1) * 64, 0:n * 128], scalar=kg8[kv * 64:(kv + 1) * 64, l:l + 1],
                    in1=tk[kv * 64:(kv + 1) * 64, 0:n * 128], op0=ALU.mult, op1=ALU.mult),
                    [tk_tt, sm_tt], [Kpad_tt[h][kv][b] for b in blocks], preads=[pk_tt])
        for i, b in enumerate(blocks):
            pv, pv_tt = alloc_R()
            for kc in range(8):
                pe(lambda kc=kc, i=i: nc.tensor.matmul(pv[:, 0:128], lhsT=H[:, kc, i * 128:(i + 1) * 128],
                                                       rhs=Wkv[:, kc, 128:256], start=(kc == 0), stop=(kc == 7)),
                   [Wkv_tt, Htt[kc]], [pv_tt], signal=(kc == 7))
            vb0 = Vaug[:, b, 0, 0:64]
            vout = bass.AP(tensor=vb0.tensor, offset=vb0.offset, ap=[list(vb0.ap[0]), [192, 2], [1, 64]])
            dve(lambda vout=vout, pv=pv: nc.vector.tensor_copy(out=vout,
                                                               in_=pv[:, 0:128].rearrange("p (a e) -> p a e", a=2)),
                [], [Vaug_tt[b]], preads=[pv_tt])

    def stage_q(l, hb, jps=(0, 1)):
        H = hT[hb]
        Htt = hT_tt[hb]
        for jp in jps:
            pq, pq_tt = alloc_R()
            for jj in range(2):
                j = jp * 2 + jj
                for kc in range(8):
                    pe(lambda kc=kc, j=j, jj=jj, pq=pq: nc.tensor.matmul(
                        pq[:, jj * 256:(jj + 1) * 256], lhsT=Win[:, kc, j * 128:(j + 1) * 128],
                        rhs=H[:, kc, :], start=(kc == 0), stop=(kc == 7)),
                       [Wsec_tt["q"], Htt[kc]], [pq_tt], signal=(jj == 1 and kc == 7))
            act(lambda jp=jp, pq=pq: nc.scalar.activation(out=qsq[jp][:], in_=pq, func=AF.Square),
                [], [qsq_tt[jp]], preads=[pq_tt])
            dve(lambda jp=jp, pq=pq: nc.vector.tensor_scalar(out=qraw[jp][:], in0=pq, scalar1=qg_col[:, l:l + 1],
                                                             scalar2=None, op0=ALU.mult),
                [sm_tt], [qraw_tt[jp]], preads=[pq_tt])


    def p2(t, l, blocks, hb, early, filler, hooks, tick=None, q_pre=False, next_q=None, tail_early=None):
        H = hT[hb]
        Htt = hT_tt[hb]

        def hook(name):
            if hooks is not None and name in hooks:
                hooks[name]()
            if tick is not None and name in ("q", "ga", "gg", "u", "vgm"):
                tick()

        def proj_pair(sec, jp, evac):
            col0 = SEC[sec][0] + jp * 256
            pg, pg_tt = alloc_R()
            for jj in range(2):
                for kc in range(8):
                    pe(lambda kc=kc, jj=jj: nc.tensor.matmul(
                        pg[:, jj * 256:(jj + 1) * 256], lhsT=Win[:, kc, col0 + jj * 128:col0 + (jj + 1) * 128],
                        rhs=H[:, kc, :], start=(kc == 0), stop=(kc == 7)),
                       [Wsec_tt[sec], Htt[kc]], [pg_tt], signal=(jj == 1 and kc == 7))
            evac(pg, pg_tt)

        def stage_qnorm():
            for jp in range(2):
                pss, pss_tt = alloc_R()
                pe(lambda jp=jp, pss=pss: nc.tensor.matmul(pss, lhsT=BD[:], rhs=qsq[jp][:], start=True, stop=True),
                   [qsq_tt[jp], idb_tt], [pss_tt])
                act(lambda jp=jp, pss=pss: nc.scalar.activation(out=tq[jp][:], in_=pss, func=AF.Ln,
                                                                bias=eps_col[:, 0:1], scale=1.0 / 64),
                    [kc_tt], [tq_tt[jp]], preads=[pss_tt])
                act(lambda jp=jp: nc.scalar.activation(out=tq[jp][:], in_=tq[jp][:], func=AF.Exp, scale=-0.5),
                    [tq_tt[jp]], [tq_tt[jp]])
                for jj in range(2):
                    j = jp * 2 + jj
                    dve(lambda j=j, jj=jj, jp=jp: nc.vector.tensor_tensor(
                        out=QnT[:, :, j, :],
                        in0=qraw[jp][:, jj * 256:(jj + 1) * 256].rearrange("p (a b) -> p a b", a=2),
                        in1=tq[jp][:, jj * 256:(jj + 1) * 256].rearrange("p (a b) -> p a b", a=2), op=ALU.mult),
                        [tq_tt[jp], qraw_tt[jp]], [QnT_tt[j]])

        def stage_ga(jp):
            proj_pair("ga", jp, lambda pg, pg_tt: act(
                lambda: nc.scalar.activation(out=sg[:, 2 * jp:2 * jp + 2, :],
                                             in_=pg.rearrange("p (a b) -> p a b", a=2), func=AF.Silu),
                [], [sg_tt[jp]], preads=[pg_tt]))

        def stage_gg(jp):
            proj_pair("gg", jp, lambda pg, pg_tt: act(
                lambda: nc.scalar.activation(out=sgg[:, 2 * jp:2 * jp + 2, :],
                                             in_=pg.rearrange("p (a b) -> p a b", a=2), func=AF.Silu),
                [], [sgg_tt[jp]], preads=[pg_tt]))

        def stage_u(jp):
            proj_pair("u", jp, lambda pg, pg_tt: dve(
                lambda: nc.vector.tensor_tensor(out=ug[:, 2 * jp:2 * jp + 2, :],
                                                in0=pg.rearrange("p (a b) -> p a b", a=2),
                                                in1=sgg[:, 2 * jp:2 * jp + 2, :], op=ALU.mult),
                [sgg_tt[jp]], [ug_tt[jp]], preads=[pg_tt]))

        def stage_vgm(i):
            c0v = SEC["vgm"][0]
            pvg, pvg_tt = alloc_R()
            for kc in range(8):
                pe(lambda kc=kc: nc.tensor.matmul(pvg, lhsT=H[:, kc, i * 128:(i + 1) * 128],
                                                  rhs=Win[:, kc, c0v:c0v + 512], start=(kc == 0), stop=(kc == 7)),
                   [Wsec_tt["vgm"], Htt[kc]], [pvg_tt], signal=(kc == 7))
            act(lambda: nc.scalar.activation(out=vsq[i][:], in_=pvg, func=AF.Square),
                [], [vsq_tt[i]], preads=[pvg_tt])
            dve(lambda: nc.vector.tensor_reduce(out=ssv[i][:], in_=vsq[i][:].rearrange("p (g e) -> p g e", e=64),
                                                axis=AX.X, op=ALU.add),
                [vsq_tt[i]], [ssv_tt[i]])
            act(lambda: nc.scalar.activation(out=ssv[i][:], in_=ssv[i][:], func=AF.Ln, bias=eps_col[:, 0:1],
                                             scale=1.0 / 64),
                [ssv_tt[i], kc_tt], [ssv_tt[i]])
            act(lambda: nc.scalar.activation(out=ssv[i][:], in_=ssv[i][:], func=AF.Exp, scale=-0.5),
                [ssv_tt[i]], [ssv_tt[i]])
            vb = vnpad[i][:, 0, 0:64]
            out3 = bass.AP(tensor=vb.tensor, offset=vb.offset, ap=[list(vb.ap[0]), [256, 4], [192, 2], [1, 64]])
            rvb = ssv[i][:, 0:1]
            in1 = bass.AP(tensor=rvb.tensor, offset=rvb.offset, ap=[list(rvb.ap[0]), [2, 4], [1, 2], [0, 64]])
            dve(lambda: nc.vector.tensor_tensor(
                out=out3, in0=pvg.rearrange("p (pr two e) -> p pr two e", two=2, e=64), in1=in1, op=ALU.mult),
                [ssv_tt[i]], [vnpad_tt[i]], preads=[pvg_tt])

        pctr = [0]

        def stage_scores(i, kv):
            b = blocks[i]
            for r in range(3):
                kb = b + r - 1
                m = kv * 3 + r
                ps_, ps_tt = alloc_R()
                rhs = QnT[:, i, kv * 2:kv * 2 + 2, :]
                pe(lambda ps_=ps_, m=m: nc.tensor.matmul(ps_, lhsT=identb[:], rhs=Ec[:, m, :], start=True, stop=False),
                   [idb_tt, const_tt], [ps_tt], signal=False)
                for h in range(2):
                    pe(lambda h=h, kb=kb, ps_=ps_, rhs=rhs: nc.tensor.matmul(
                        ps_[:, h * 256:(h + 1) * 256], lhsT=Kpad[h][kv][:, kb * 128:(kb + 1) * 128], rhs=rhs,
                        start=False, stop=(h == 1)),
                       [Kpad_tt[h][kv][kb], QnT_tt[kv * 2], QnT_tt[kv * 2 + 1]], [ps_tt], signal=(h == 1))
                col = t * NB + kb
                act(lambda ps_=ps_, m=m, col=col: nc.scalar.activation(out=PT[m][:], in_=ps_, func=AF.Exp,
                                                                       bias=kmask[:, col:col + 1], scale=1.0),
                    [sm_tt], [PT_tt[m]], preads=[ps_tt])

        pvbank = {}

        def stage_pv(i, kv):
            b = blocks[i]
            ppv, ppv_tt = alloc_R()
            pvbank[kv] = (ppv, ppv_tt)
            for r in range(3):
                kb = b + r - 1
                m = kv * 3 + r
                pe(lambda r=r, kb=kb, m=m: nc.tensor.matmul(
                    ppv, lhsT=Vaug[:, kb, kv, :], rhs=PT[m][:], start=(r == 0), stop=(r == 2)),
                   [Vaug_tt[kb], PT_tt[m]], [ppv_tt], signal=(r == 2))
            rd = rden[i]
            dlo = 64 * (1 - kv)
            esb = es_sl[dlo:dlo + 64, l, kv * 4:kv * 4 + 4]
            dve(lambda: nc.vector.tensor_tensor(
                out=rd[dlo:dlo + 64, :].rearrange("p (a b) -> p a b", a=4),
                in0=ppv[dlo:dlo + 64, :].rearrange("p (a b) -> p a b", a=4), in1=_bc_last(esb, 128), op=ALU.add),
                [const_tt], [rden_tt[i]], preads=[ppv_tt])
            if kv == 0:
                return
            act(lambda: nc.scalar.activation(out=rd[:, :], in_=rd[:, :], func=AF.Ln), [rden_tt[i]], [rden_tt[i]])
            act(lambda: nc.scalar.activation(out=rd[:, :], in_=rd[:, :], func=AF.Exp, scale=-1.0),
                [rden_tt[i]], [rden_tt[i]])
            for kv2 in range(2):
                pp, pp_tt = pvbank[kv2]
                nlo = 64 * kv2
                dl2 = 64 * (1 - kv2)
                for h in range(2):
                    dve(lambda h=h, kv2=kv2, pp=pp, nlo=nlo, dl2=dl2: nc.vector.tensor_tensor(
                        out=ya[h * 64:(h + 1) * 64, kv2 * 2:kv2 * 2 + 2, :],
                        in0=pp[nlo:nlo + 64, h * 256:(h + 1) * 256].rearrange("p (a b) -> p a b", a=2),
                        in1=rd[dl2:dl2 + 64, h * 256:(h + 1) * 256].rearrange("p (a b) -> p a b", a=2), op=ALU.mult),
                        [rden_tt[i]], [ya_tt[kv2][h]], preads=[pp_tt])

        def stage_ya(i):
            dve(lambda: nc.vector.tensor_tensor(out=yT[i][:, 0:4, :], in0=ya[:],
                                                in1=sg[:, :, i * 128:(i + 1) * 128], op=ALU.mult),
                [ya_tt[0][0], ya_tt[0][1], ya_tt[1][0], ya_tt[1][1]] + sg_tt, [yTA_tt[i]])

        def stage_gmlp(i):
            psv, psv_tt = alloc_R()
            for pr in range(4):
                o = psv[:, pr * 128:(pr + 1) * 128]
                pe(lambda pr=pr, o=o: nc.tensor.matmul(o, lhsT=vnpad[i][:, 2 * pr, :], rhs=WsT[:, l, 2 * pr, :],
                                                       start=True, stop=False),
                   [vnpad_tt[i], wsT_tt], [psv_tt], signal=False)
                pe(lambda pr=pr, o=o: nc.tensor.matmul(o, lhsT=vnpad[i][:, 2 * pr + 1, :],
                                                       rhs=WsT[:, l, 2 * pr + 1, :], start=False, stop=False),
                   [vnpad_tt[i], wsT_tt], [psv_tt], signal=False)
                pe(lambda pr=pr, o=o: nc.tensor.matmul(o, lhsT=Lb[:], rhs=Rb[:, l, pr, :], start=False, stop=True),
                   [const_tt, rb_tt], [psv_tt], signal=(pr == 3))
            dve(lambda: nc.vector.tensor_tensor(
                out=yT[i][:, 4:8, :], in0=psv.rearrange("p (a b) -> p a b", a=4), in1=ug[:, :, i * 128:(i + 1) * 128],
                op=ALU.mult),
                ug_tt, [yTM_tt[i]], preads=[psv_tt])

        def stage_out(i, hfs=(0, 1)):
            b = blocks[i]
            for hf in hfs:
                po, po_tt = alloc_R()
                for c in range(8):
                    pe(lambda c=c, hf=hf, po=po: nc.tensor.matmul(
                        po, lhsT=yT[i][:, c, :], rhs=Wout[:, c, hf * 512:(hf + 1) * 512], start=(c == 0),
                        stop=(c == 7)),
                       [yTA_tt[i] if c < 4 else yTM_tt[i], Wout_tt[c // 4]], [po_tt], signal=(c == 7))
                xs = xres[b - 1][:, hf * 512:(hf + 1) * 512]
                gsl = gate_bc[:, l, hf * 512:(hf + 1) * 512]
                dve(lambda po=po, gsl=gsl: nc.vector.tensor_tensor(out=po, in0=po, in1=gsl, op=ALU.mult),
                    gate_tt[l][2 * hf:2 * hf + 2], [po_tt])
                dve(lambda po=po, xs=xs: nc.vector.tensor_tensor(out=xs, in0=po, in1=xs, op=ALU.add),
                    [xres_tt[b - 1][hf]], [xres_tt[b - 1][hf]], preads=[po_tt])
            if l == DEPTH - 1 and 1 in hfs:
                dma(SP, xres_ds[b - 1], out_d[t, (b - 2) * 128:(b - 1) * 128, :], xres[b - 1][:],
                    reads=xres_tt[b - 1])
                if t + 1 < NT:
                    dma(SP, xres_ds[b - 1], xres[b - 1][:], xin[t + 1, b * 128:(b + 1) * 128, :],
                        writes=xres_tt[b - 1])

        if early is not None:
            early()
        if not q_pre:
            stage_q(l, hb)
            stage_qnorm()
        hook("q")
        stage_ga(0)
        stage_ga(1)
        hook("ga")
        stage_gg(0)
        stage_scores(0, 0)
        stage_gg(1)
        hook("gg")
        stage_scores(0, 1)
        stage_u(0)
        if filler is not None:
            filler()
        hook("p1")
        stage_pv(0, 0)
        stage_u(1)
        stage_pv(0, 1)
        stage_ya(0)
        hook("u")
        stage_scores(1, 0)
        stage_vgm(0)
        stage_scores(1, 1)
        stage_vgm(1)
        hook("vgm")
        stage_pv(1, 0)
        stage_gmlp(0)
        stage_pv(1, 1)
        stage_ya(1)
        stage_gmlp(1)
        hook("pre_out")
        if tail_early is not None:
            tail_early()
        stage_out(0, (0,))
        if next_q is not None:
            next_q((0,))
        stage_out(0, (1,))
        if next_q is not None:
            next_q((1,))
        stage_out(1)
        if next_q is not None:
            stage_qnorm()
        hook("out")

    load_wkv(0)
    class _V:
        def __init__(self, ap):
            self.ap = ap

        def __getitem__(self, idx):
            return self.ap[idx]

    abufs = {}
    for pc in range(2, 8):
        xi = 4 + (pc - 2)
        abufs[pc] = (_V(xres[xi][:].bitcast(BF16).rearrange("p (k n) -> p k n", k=8)), xres_tt[xi],
                     DSem(sem(f"d_ast{pc}")))
    for pc in (10, 11):
        abufs[pc] = (abufs[pc - 4][0], abufs[pc - 4][1], DSem(sem(f"d_ast{pc}")))
    ada0 = ada_tasks(0, abufs)
    for tk_ in ada0[:9]:
        tk_()
    for k in ("q", "ga", "gg", "u", "vgm"):
        load_wsec(0, k)
    load_wout(0)
    for l in range(DEPTH):
        dma(POOL, wsT_ds, WsT[:, l], wsT_d[l], writes=[wsT_tt])
    gate0 = ada0[9:]
    bg = list(ada_tasks(1))

    def drip(n=1):
        for _ in range(n):
            if bg:
                bg.pop(0)()

    def run_all(lst):
        while lst:
            lst.pop(0)()

    def layer_geom(l):
        lo, hi = l, NB - l
        sbs = [(b, b + 1) for b in range(lo + 1, hi - 1, 2)]
        return lo, hi, sbs

    def layer_head(t, l, part=0):
        lo, hi, sbs = layer_geom(l)
        halo = (l == 0)
        nxt_tile = (l == DEPTH - 1 and t + 1 < NT)
        if part in (0, 1):
            p1a(t, l, [lo], halo)
            if nxt_tile:
                dma(SP, xres_ds[lo - 1], xres[lo - 1][:], xin[t + 1, lo * 128:(lo + 1) * 128, :],
                    writes=xres_tt[lo - 1])
        if part == 1:
            return
        p1b(t, l, [lo], 1)
        p1a(t, l, list(sbs[0]), False)
        p1b(t, l, list(sbs[0]), 0)
        if first_sb_of(t, l) == 1:
            p1a(t, l, list(sbs[1]), False)
            p1b(t, l, list(sbs[1]), 1)
            for j in range(2):
                dve(lambda j=j: nc.vector.tensor_copy(out=xres[j][:], in_=stash[j]), [wst_tt[j]], xres_tt[j])
        s0h = first_sb_of(t, l)
        nsb_h = len(sbs)
        if s0h + 1 < nsb_h:
            p1a(t, l, list(sbs[s0h + 1]), False)
        else:
            p1a(t, l, [hi - 1], halo)

    def first_sb_of(t, l):
        return 1 if (l == 0 and t > 0) else 0

    stash = [wst[j][:].rearrange("p a b -> p (a b)").bitcast(F32) for j in range(2)]
    layer_head(0, 0)
    late_setup()
    for b in range(5, NB - 3):
        dma(SP, xres_ds[b - 1], xres[b - 1][:], xin[0, b * 128:(b + 1) * 128, :], writes=xres_tt[b - 1])

    def late_x_loads():
        for b in range(NB - 3, NB - 1):
            dma(SP, xres_ds[b - 1], xres[b - 1][:], xin[0, b * 128:(b + 1) * 128, :], writes=xres_tt[b - 1])
    q_pre = [False]
    for t in range(NT):
        for l in range(DEPTH):
            lo, hi, sbs = layer_geom(l)
            nsb = len(sbs)
            halo = (l == 0)
            last_layer = (t == NT - 1 and l == DEPTH - 1)
            nl_ = (l + 1) % DEPTH
            nt_ = t if l + 1 < DEPTH else t + 1
            nxt_tile = (l == DEPTH - 1 and t + 1 < NT)
            s0 = first_sb_of(t, l)
            for s in range(s0, nsb):
                if s + 1 < nsb:
                    nblocks, nhb, nhalo = list(sbs[s + 1]), (s + 1) % 2, False
                else:
                    nblocks, nhb, nhalo = [hi - 1], nsb % 2, halo

                def mk_early(s2):
                    if s2 + 1 < nsb:
                        nb2, nh2 = list(sbs[s2 + 1]), False
                    else:
                        nb2, nh2 = [hi - 1], halo

                    def early():
                        p1a(t, l, nb2, nh2)
                        if nxt_tile and s2 == nsb - 1:
                            bb = hi - 1
                            dma(SP, xres_ds[bb - 1], xres[bb - 1][:], xin[t + 1, bb * 128:(bb + 1) * 128, :],
                                writes=xres_tt[bb - 1])
                    return early

                early = None
                tail_early = mk_early(s + 1) if s + 1 < nsb else None

                filler = (lambda nblocks=nblocks, nhb=nhb: p1b(t, l, nblocks, nhb))
                hooks = {}
                if s == nsb - 1 and not last_layer:
                    hooks = {k: (lambda k=k: load_wsec(nl_, k)) for k in ("q", "ga", "gg", "u", "vgm")}

                    def after_vgm(nl_=nl_, nt_=nt_):
                        load_wsec(nl_, "vgm")
                        drip(100)
                        layer_head(nt_, nl_, part=2)

                    def after_u(nl_=nl_, nt_=nt_):
                        load_wsec(nl_, "u")
                        drip(100)
                        layer_head(nt_, nl_, part=1)

                    hooks["p1"] = (lambda nl_=nl_: load_wkv(nl_))
                    hooks["u"] = after_u
                    hooks["vgm"] = after_vgm
                    hooks["out"] = (lambda: load_wout(nl_))
                first_sb = (t == 0 and l == 0 and s == 0)
                if first_sb:
                    hooks["gg"] = gate0.pop(0)
                    hooks["u"] = gate0.pop(0)
                    hooks["vgm"] = gate0.pop(0)
                    g_last = gate0.pop(0)
                    hooks["pre_out"] = (lambda g_last=g_last: (g_last(), late_x_loads()))
                tick = drip if (t == 0 and l == 0 and s > 0) else None
                if s + 1 < nsb:
                    next_q = (lambda jps, s=s: stage_q(l, (s + 1) % 2, jps))
                elif not last_layer:
                    next_q = (lambda jps, nl_=nl_, hb_=first_sb_of(nt_, nl_) % 2: stage_q(nl_, hb_, jps))
                else:
                    next_q = None
                p2(t, l, sbs[s], s % 2, early, filler, hooks, tick=tick, q_pre=q_pre[0], next_q=next_q,
                   tail_early=tail_early)
                q_pre[0] = next_q is not None
            if l == 0 and t + 1 < NT:
                for j in range(2):
                    bb = NB - 3 + j
                    dve(lambda j=j, bb=bb: nc.vector.tensor_copy(out=stash[j], in_=xres[bb - 1][:]),
                        xres_tt[bb - 1], [wst_tt[j]])
    for ds in xres_ds:
        SP.h.wait_ge(ds.sem, ds.n)
    return nc, es


def _host_constants():
    slopes = 2.0 ** (-8.0 * np.arange(1, 9, dtype=np.float64) / 8)
    kl = np.arange(128)[:, None]
    ql = np.arange(128)[None, :]
    E = np.zeros((128, 6, 4, 128), np.float64)
    for kv in range(2):
        for r in range(3):
            if r == 0:
                dist = 128 + ql - kl
            elif r == 1:
                dist = np.abs(ql - kl)
            else:
                dist = 128 + kl - ql
            valid = dist <= 128
            for s in range(4):
                c = kv * 2 + (s % 2)
                head = 2 * c + (s // 2)
                E[:, kv * 3 + r, s, :] = np.where(valid, -slopes[head] * dist, MASKV)
    E = E.reshape(128, 6, 512).astype(np.float32).astype(ml_dtypes.bfloat16)
    bd = np.zeros((128, 128), np.float32)
    bd[0:64, 0:64] = 1.0
    bd[64:128, 64:128] = 1.0
    lb = np.zeros((128, 128), np.float32)
    lb[0, 0:64] = 1.0
    lb[64, 0:64] = 1.0
    lb[32, 64:128] = 1.0
    lb[96, 64:128] = 1.0
    return E, bd.astype(ml_dtypes.bfloat16), lb.astype(ml_dtypes.bfloat16), np.eye(128, dtype=np.float32)


_SLOT_HEADS = [0, 2, 1, 3, 4, 6, 5, 7]


def make_in_maps(x, c, w_ada, b_ada, norm_gain, w_in, q_gain, k_gain, sink, w_s, b_s, w_out):
    f = lambda a: np.ascontiguousarray(np.asarray(a), dtype=np.float32)
    x, c, w_ada, b_ada, norm_gain, w_in = f(x), f(c), f(w_ada), f(b_ada), f(norm_gain), f(w_in)
    q_gain, k_gain, sink, w_s, b_s, w_out = f(q_gain), f(k_gain), f(sink), f(w_s), f(b_s), f(w_out)
    E, bd, lb, ident = _host_constants()
    shared = {
        "w_ada": w_ada,
        "bada_col": np.ascontiguousarray(b_ada[:, 0:2048].reshape(DEPTH, 16, 128).transpose(0, 2, 1)),
        "bada_gate": np.ascontiguousarray(b_ada[:, 2048:3072].reshape(DEPTH, 1, D)),
        "ng_col": np.ascontiguousarray(norm_gain.reshape(DEPTH, 8, 128).transpose(0, 2, 1)),
        "w_in": w_in,
        "qg_col": np.ascontiguousarray(np.concatenate([q_gain, q_gain], axis=1).T),
        "kg_col": np.ascontiguousarray(np.concatenate([k_gain, k_gain], axis=1).T),
        "sinkp": np.ascontiguousarray(sink[:, _SLOT_HEADS].reshape(DEPTH, 1, 8)),
        "wsT": np.ascontiguousarray(w_s.transpose(0, 3, 1, 2)),
        "bs": b_s,
        "w_out": w_out,
        "ident": ident,
        "bd": bd,
        "identb": ident.astype(ml_dtypes.bfloat16),
        "lb": lb,
        "econst": E,
    }
    nblk_seq = SEQ // 128
    in_maps = []
    for core in range(NCORES):
        b = core // 4
        ci = core % 4
        xin = np.zeros((NT, NB * 128, D), np.float32)
        km = np.zeros((128, NT * NB), np.float32)
        for t in range(NT):
            g0 = ci * 16 + t * OWN - 2
            for lb_ in range(NB):
                g = g0 + lb_
                if 0 <= g < nblk_seq:
                    xin[t, lb_ * 128:(lb_ + 1) * 128] = x[b, g * 128:(g + 1) * 128]
                else:
                    km[:, t * NB + lb_] = MASKV
        m = dict(shared)
        m["xin"] = xin
        m["kmask"] = km
        m["ccol"] = np.ascontiguousarray(c[b].reshape(8, 128).T)
        in_maps.append(m)
    return in_maps


_CACHE = {}


def kernel(x, c, w_ada, b_ada, norm_gain, w_in, q_gain, k_gain, sink, w_s, b_s, w_out):
    in_maps = make_in_maps(x, c, w_ada, b_ada, norm_gain, w_in, q_gain, k_gain, sink, w_s, b_s, w_out)
    if "nc" not in _CACHE:
        _CACHE["nc"] = build_program()
    nc, _es = _CACHE["nc"]
    res = run_bass_kernel_spmd(nc, in_maps, core_ids=list(range(NCORES)))
    out = np.zeros((2, SEQ, D), np.float32)
    for core in range(NCORES):
        b = core // 4
        ci = core % 4
        o = res.results[core]["out"]
        for t in range(NT):
            g = ci * 16 + t * OWN
            out[b, g * 128:(g + OWN) * 128] = o[t]
    return out
```

```python
import contextlib
import numpy as np
import ml_dtypes
import concourse.bass as bass
import concourse.mybir as mybir
from concourse.bass_utils import run_bass_kernel_spmd

F32 = mybir.dt.float32
BF16 = mybir.dt.bfloat16
AF = mybir.ActivationFunctionType
ALU = mybir.AluOpType
AX = mybir.AxisListType

D = 1024
SEQ = 8192
NCORES = 8
DEPTH = 2
D_IN = 2816
NT = 2
OWN = 16 // NT
NB = OWN + 4
NXR = NB - 2
EPS = 1e-6
MASKV = -30000.0


class TT:
    __slots__ = ("name", "w", "r")

    def __init__(self, name):
        self.name = name
        self.w = None
        self.r = []


class Eng:
    def __init__(self, handle, name, sem):
        self.h = handle
        self.name = name
        self.sem = sem
        self.n = 0
        self.seen = {}


class DSem:
    def __init__(self, sem):
        self.sem = sem
        self.n = 0


def _bc_last(ap, n):
    return bass.AP(tensor=ap.tensor, offset=ap.offset, ap=[list(x) for x in ap.ap] + [[0, n]])


def _bc_col(col, n):
    return bass.AP(tensor=col.tensor, offset=col.offset, ap=[list(col.ap[0]), [0, n]])


def build_program():
    nc = bass.Bass("TRN2", target_bir_lowering=False)
    es = contextlib.ExitStack()

    def dram(name, shape, dt, kind):
        return nc.dram_tensor(name, list(shape), dt, kind=kind).ap()

    IN = "ExternalInput"
    xin = dram("xin", [NT, NB * 128, D], F32, IN)
    kmask_d = dram("kmask", [128, NT * NB], F32, IN)
    ccol_d = dram("ccol", [128, 8], F32, IN)
    w_ada = dram("w_ada", [DEPTH, D, 3 * D], F32, IN)
    bada_col_d = dram("bada_col", [DEPTH, 128, 16], F32, IN)
    bada_gate_d = dram("bada_gate", [DEPTH, 1, D], F32, IN)
    ng_col_d = dram("ng_col", [DEPTH, 128, 8], F32, IN)
    w_in = dram("w_in", [DEPTH, D, D_IN], F32, IN)
    qg_col_d = dram("qg_col", [128, DEPTH], F32, IN)
    kg_col_d = dram("kg_col", [128, DEPTH], F32, IN)
    sink_d = dram("sinkp", [DEPTH, 1, 8], F32, IN)
    wsT_d = dram("wsT", [DEPTH, 128, 8, 128], F32, IN)
    bs_d = dram("bs", [DEPTH, 8, 128], F32, IN)
    w_out = dram("w_out", [DEPTH, D, D], F32, IN)
    ident_d = dram("ident", [128, 128], F32, IN)
    bd_d = dram("bd", [128, 128], BF16, IN)
    identb_d = dram("identb", [128, 128], BF16, IN)
    lb_d = dram("lb", [128, 128], BF16, IN)
    e_d = dram("econst", [128, 6, 512], BF16, IN)
    out_d = dram("out", [NT, OWN * 128, D], F32, "ExternalOutput")

    def sb(name, shape, dt):
        return es.enter_context(nc.sbuf_tensor("sb_" + name, list(shape), dt))

    def sem(name):
        return es.enter_context(nc.semaphore(name))

    PE = Eng(nc.tensor, "pe", sem("s_pe"))
    ACT = Eng(nc.scalar, "act", sem("s_act"))
    DVE = Eng(nc.vector, "dve", sem("s_dve"))
    POOL = Eng(nc.gpsimd, "pool", sem("s_pool"))
    SP = Eng(nc.sync, "sp", sem("s_sp"))

    def _waits(eng, reads, writes, preads=()):
        need = {}
        deps = []
        for t in reads:
            if t.w is not None:
                deps.append(t.w)
        for t in preads:
            if t.w is not None:
                deps.append(t.w)
            for m in t.r:
                if m[2] != eng.name:
                    deps.append(m)
        for t in writes:
            if t.w is not None:
                deps.append(t.w)
            deps.extend(t.r)
        for (s, val, en) in deps:
            if en == eng.name and eng.name == "pe":
                continue
            if eng.seen.get(s, 0) >= val:
                continue
            if need.get(s, (None, 0))[1] < val:
                need[s] = (s, val)
        for s, val in need.values():
            eng.h.wait_ge(s, val)
            eng.seen[s] = val

    def emit(eng, fn, reads=(), writes=(), signal=True, preads=()):
        _waits(eng, reads, writes, preads)
        inst = fn()
        if signal:
            eng.n += 1
            inst.then_inc(eng.sem, 1)
            mark = (eng.sem, eng.n, eng.name)
        else:
            mark = (eng.sem, eng.n + 1, eng.name)
        for t in reads:
            t.r.append(mark)
        for t in preads:
            t.r.append(mark)
        for t in writes:
            t.w = mark
            t.r = []
        return inst

    def dma(q, ds, out, in_, reads=(), writes=(), nodep=False):
        if not nodep:
            _waits(q, reads, writes)
        inst = q.h.dma_start(out=out, in_=in_)
        ds.n += 16
        inst.then_inc(ds.sem, 16)
        mark = (ds.sem, ds.n, "dma")
        for t in reads:
            t.r.append(mark)
        for t in writes:
            t.w = mark
            t.r = []

    def pe(fn, reads, writes, signal=True):
        return emit(PE, fn, reads, writes, signal)

    def act(fn, reads, writes, preads=()):
        return emit(ACT, fn, reads, writes, True, preads)

    def dve(fn, reads, writes, preads=()):
        return emit(DVE, fn, reads, writes, True, preads)

    def pool(fn, reads, writes):
        return emit(POOL, fn, reads, writes)

    xres = [sb(f"xres{i}", [128, D], F32) for i in range(NXR)]
    xres_tt = [[TT(f"xres{i}_{h}") for h in range(2)] for i in range(NXR)]
    xres_ds = [DSem(sem(f"d_x{i}")) for i in range(NXR)]
    xr = [sb(f"xr{i}", [128, D], BF16) for i in range(2)]
    xr_tt = [TT(f"xr{i}") for i in range(2)]
    xh = sb("xh", [128, D], F32)
    xh_tt = TT("xh")
    xh_ds = DSem(sem("d_xh"))
    hT = [sb(f"hT{i}", [128, 8, 256], BF16) for i in range(2)]
    hT_tt = [[TT(f"hT{i}_{c}") for c in range(8)] for i in range(2)]
    Win = sb("Win", [128, 8, 2560], BF16)
    SEC = {"q": (0, 0), "ga": (512, 768), "u": (1024, 1280), "vgm": (1536, 1792), "gg": (2048, 2304)}
    Wsec_tt = {k: TT("Win_" + k) for k in SEC}
    Wsec_ds = {k: DSem(sem("d_w" + k)) for k in SEC}
    Wkv = sb("Wkv", [128, 8, 256], BF16)
    Wkv_tt = TT("Wkv")
    Wkv_ds = DSem(sem("d_wkv"))
    Wout = sb("Wout", [128, 8, D], BF16)
    Wout_tt = [TT(f"Wout{h}") for h in range(2)]
    Wout_ds = [DSem(sem(f"d_wo{h}")) for h in range(2)]
    Kpad = [[sb(f"Kpad{h}{kv}", [128, NB * 128], BF16) for kv in range(2)] for h in range(2)]
    Kpad_tt = [[[TT(f"Kpad{h}{kv}_{b}") for b in range(NB)] for kv in range(2)] for h in range(2)]
    Vaug = sb("Vaug", [128, NB, 2, 128], BF16)
    Vaug_tt = [TT(f"Vaug{b}") for b in range(NB)]
    Ec = sb("Ec", [128, 6, 512], BF16)
    ident = sb("ident", [128, 128], F32)
    BD = sb("BD", [128, 128], BF16)
    identb = sb("identb", [128, 128], BF16)
    Lb = sb("Lb", [128, 128], BF16)
    WsT = sb("WsT", [128, DEPTH, 8, 128], BF16)
    Rb = sb("Rb", [128, DEPTH, 4, 128], BF16)
    gate_bc = sb("gate_bc", [128, DEPTH, D], F32)
    gate_tt = [[TT(f"gate{l}_{i}") for i in range(4)] for l in range(DEPTH)]
    gate_ds = [DSem(sem(f"d_gate{l}")) for l in range(DEPTH)]
    wst = [sb(f"wst{i}", [128, 8, 256], BF16) for i in range(2)]
    wst_tt = [TT(f"wst{i}") for i in range(2)]
    wst_ds = [DSem(sem(f"d_wst{i}")) for i in range(2)]
    condbc = sb("condbc", [128, 8, 128], BF16)
    cond_bf = sb("cond_bf", [128, 8], BF16)
    ccol = sb("ccol", [128, 8], F32)
    kmask = sb("kmask", [128, NT * NB], F32)
    bada_col = sb("bada_col", [128, DEPTH, 16], F32)
    ng_col = sb("ng_col", [128, DEPTH, 8], F32)
    adacol = sb("adacol", [128, DEPTH, 16], F32)
    Gc = sb("Gc", [128, DEPTH, 8], F32)
    ada_tt = [TT("ada0"), TT("ada1")]
    qg_col = sb("qg_col", [128, DEPTH], F32)
    kg_col = sb("kg_col", [128, DEPTH], F32)
    kg8 = sb("kg8", [128, DEPTH], F32)
    es_sl = sb("es_sl", [128, DEPTH, 8], F32)
    mhalf = sb("mhalf", [128, 1], F32)
    eps_col = sb("eps_col", [128, 1], F32)
    const_tt = TT("consts")
    const_ds = DSem(sem("d_const"))
    ssb = [sb(f"ssb{i}", [128, 2], F32) for i in range(2)]
    ssb_tt = [[TT(f"ssb{i}_{k}") for k in range(2)] for i in range(2)]
    msb = [sb(f"msb{i}", [128, 2], F32) for i in range(2)]
    msb_tt = [TT(f"msb{i}") for i in range(2)]
    ksq = sb("ksq", [128, 256], BF16)
    ksq_tt = TT("ksq")
    tk = sb("tk", [128, 256], F32)
    tk_tt = TT("tk")
    qsq = [sb(f"qsq{i}", [128, 512], BF16) for i in range(2)]
    qsq_tt = [TT(f"qsq{i}") for i in range(2)]
    tq = [sb(f"tq{i}", [128, 512], F32) for i in range(2)]
    tq_tt = [TT(f"tq{i}") for i in range(2)]
    QnT = sb("QnT", [128, 2, 4, 128], BF16)
    qraw = [sb(f"qraw{i}", [128, 512], F32) for i in range(2)]
    qraw_tt = [TT(f"qraw{i}") for i in range(2)]
    QnT_tt = [TT(f"QnT{j}") for j in range(4)]
    sg = sb("sg", [128, 4, 256], BF16)
    sg_tt = [TT(f"sg{j}") for j in range(2)]
    sgg = sb("sgg", [128, 4, 256], BF16)
    sgg_tt = [TT(f"sgg{j}") for j in range(2)]
    ug = sb("ug", [128, 4, 256], BF16)
    ug_tt = [TT(f"ug{j}") for j in range(2)]
    vsq = [sb(f"vsq{i}", [128, 512], BF16) for i in range(2)]
    vsq_tt = [TT(f"vsq{i}") for i in range(2)]
    ssv = [sb(f"ssv{i}", [128, 8], F32) for i in range(2)]
    ssv_tt = [TT(f"ssv{i}") for i in range(2)]
    vnpad = [sb(f"vnpad{i}", [128, 8, 128], BF16) for i in range(2)]
    vnpad_tt = [TT(f"vnpad{i}") for i in range(2)]
    PT = [sb(f"PT{i}", [128, 512], BF16) for i in range(6)]
    PT_tt = [TT(f"PT{i}") for i in range(6)]
    rden = [sb(f"rden{i}", [128, 512], F32) for i in range(2)]
    rden_tt = [TT(f"rden{i}") for i in range(2)]
    ya = sb("ya", [128, 4, 128], F32)
    ya_tt = [[TT(f"ya{kv}{h}") for h in range(2)] for kv in range(2)]
    yT = [sb(f"yT{i}", [128, 8, 128], BF16) for i in range(2)]
    yTA_tt = [TT(f"yTA{i}") for i in range(2)]
    yTM_tt = [TT(f"yTM{i}") for i in range(2)]

    psum = [es.enter_context(nc.psum_tensor(f"ps{i}", [128, 512], F32)) for i in range(8)]
    R_ap = [psum[i][:, :] for i in range(8)]
    R_tt = [TT(f"psR{i}") for i in range(8)]
    rot = [0]

    def alloc_R():
        i = rot[0] % 8
        rot[0] += 1
        return R_ap[i], R_tt[i]

    cond_tt, kc_tt, adac_tt, idb_tt, sm_tt = TT("cond"), TT("kc"), TT("adac"), TT("idb"), TT("sm")
    cond_ds, adac_ds, idb_ds, sm_ds = (DSem(sem("d_cond")), DSem(sem("d_adac")), DSem(sem("d_idb")),
                                       DSem(sem("d_sm")))
    dma(SP, cond_ds, ccol[:], ccol_d, writes=[cond_tt])
    dma(SP, xh_ds, xh[:], xin[0, 0:128, :], writes=[xh_tt])
    pre_issued = {(0, 0)}
    for b in range(1, 5):
        dma(SP, xres_ds[b - 1], xres[b - 1][:], xin[0, b * 128:(b + 1) * 128, :], writes=xres_tt[b - 1])
    dma(SP, idb_ds, identb[:], identb_d, writes=[idb_tt], nodep=True)
    dma(SP, idb_ds, BD[:], bd_d, writes=[idb_tt], nodep=True)
    dma(SP, adac_ds, bada_col[:], bada_col_d.rearrange("l p c -> p l c"), writes=[adac_tt], nodep=True)
    dma(SP, adac_ds, ng_col[:], ng_col_d.rearrange("l p c -> p l c"), writes=[adac_tt], nodep=True)
    dma(SP, sm_ds, kg_col[:], kg_col_d, writes=[sm_tt], nodep=True)
    dma(SP, sm_ds, qg_col[:], qg_col_d, writes=[sm_tt], nodep=True)
    dma(SP, sm_ds, kmask[:], kmask_d, writes=[sm_tt], nodep=True)

    def cload(dst, src):
        dma(SP, const_ds, dst, src, writes=[const_tt], nodep=True)

    cload(Ec[:], e_d)
    cload(Lb[:], lb_d)
    for l in range(DEPTH):
        cload(es_sl[:, l, :], sink_d[l].partition_broadcast(128))
    dve(lambda: nc.vector.memset(mhalf[:], -0.5), [], [kc_tt])
    dve(lambda: nc.vector.memset(eps_col[:], EPS), [], [kc_tt])
    wsT_ds = DSem(sem("d_wsT"))
    wsT_tt = TT("wsT")
    rb_tt = TT("rb")
    bs_ds = DSem(sem("d_bs"))
    act(lambda: nc.scalar.activation(out=cond_bf[:], in_=ccol[:], func=AF.Silu), [cond_tt], [cond_tt])
    dve(lambda: nc.vector.tensor_copy(out=condbc[:], in_=_bc_last(cond_bf[:], 128)), [cond_tt], [cond_tt])
    dve(lambda: nc.vector.tensor_scalar(out=kg8[:], in0=kg_col[:], scalar1=0.125, scalar2=None,
                                        op0=ALU.mult), [sm_tt], [sm_tt])

    stg_l = [gate_bc[:, 0, 512 * l:512 * (l + 1)].rearrange("p (a b) -> p a b", a=4) for l in range(DEPTH)]
    stg2 = gate_bc[:, 1, 0:512].rearrange("p (a b) -> p a b", a=4)
    stg_tts = [rb_tt] + gate_tt[0] + gate_tt[1]
    stgz_tt = TT("stgz")
    bsrow_tt = []
    dve(lambda: nc.vector.memset(gate_bc[:, 0, :], 0.0), [], stg_tts + [stgz_tt])
    for l in range(DEPTH):
        bsv = bs_d[l].rearrange("(pr two) t -> two pr t", two=2)
        for (prt, two) in ((0, 0), (32, 1), (64, 0), (96, 1)):
            tt_ = TT(f"bsrow{l}_{prt}")
            bsrow_tt.append(tt_)
            dma(SP, bs_ds, stg_l[l][prt:prt + 1, :, :], bsv[two:two + 1], reads=[stgz_tt], writes=[tt_])

    def late_setup():
        act(lambda: nc.scalar.activation(out=es_sl[:], in_=es_sl[:], func=AF.Exp), [const_tt], [const_tt])
        for l in range(DEPTH):
            dve(lambda l=l: nc.vector.tensor_copy(out=Rb[:, l], in_=stg_l[l]), stg_tts + bsrow_tt, stg_tts)
            dve(lambda l=l: nc.vector.tensor_tensor(out=stg2[64:128], in0=stg_l[l][64:128], in1=Rb[64:128, l],
                                                    op=ALU.subtract), stg_tts, stg_tts)
            dve(lambda l=l: nc.vector.tensor_copy(out=Rb[64:128, l], in_=stg2[64:128]), stg_tts, stg_tts)
        for l in range(DEPTH):
            dma(SP, gate_ds[l], gate_bc[:, l, :], bada_gate_d[l].partition_broadcast(128),
                writes=[rb_tt] + gate_tt[l])

    def load_wkv(l):
        dma(POOL, Wkv_ds, Wkv[:], w_in[l][:, 512:768].rearrange("(kc p) n -> p kc n", p=128),
            writes=[Wkv_tt])

    def load_wsec(l, k):
        c0, s0 = SEC[k]
        dma(POOL, Wsec_ds[k], Win[:, :, c0:c0 + 512],
            w_in[l][:, s0:s0 + 512].rearrange("(kc p) n -> p kc n", p=128), writes=[Wsec_tt[k]])

    def load_wout(l):
        for h in range(2):
            dma(POOL, Wout_ds[h], Wout[:, 4 * h:4 * h + 4, :],
                w_out[l][512 * h:512 * h + 512, :].rearrange("(kc p) n -> p kc n", p=128), writes=[Wout_tt[h]])

    tasks_piece_dma = {}

    def ada_tasks(l, bufs=None):
        def pbuf(pc):
            if bufs is not None and pc in bufs:
                return bufs[pc]
            k = pc % 2
            return wst[k], [wst_tt[k]], wst_ds[k]

        def piece_dma(pc):
            w_, tts_, ds_ = pbuf(pc)
            dma(POOL, ds_, w_[:], w_ada[l][:, pc * 256:(pc + 1) * 256].rearrange("(kc p) n -> p kc n", p=128),
                writes=tts_)

        def task(pc):
            if bufs is None:
                if pc + 1 < 12:
                    piece_dma(pc + 1)
            else:
                for q in {2: (8, 9)}.get(pc, ()):
                    piece_dma(q)
            wk, wk_tts, _ = pbuf(pc)
            if pc < 8:
                pada, pada_tt = alloc_R()
                for dc in range(2):
                    for kc in range(8):
                        pe(lambda kc=kc, dc=dc: nc.tensor.matmul(
                            pada[:, dc:dc + 1], lhsT=wk[:, kc, dc * 128:(dc + 1) * 128],
                            rhs=cond_bf[:, kc:kc + 1], start=(kc == 0), stop=(kc == 7)),
                           wk_tts + [cond_tt], [pada_tt], signal=(kc == 7 and dc == 1))
                dve(lambda: nc.vector.tensor_tensor(out=adacol[:, l, 2 * pc:2 * pc + 2], in0=pada[:, 0:2],
                                                    in1=bada_col[:, l, 2 * pc:2 * pc + 2], op=ALU.add),
                    [adac_tt], [ada_tt[l]], preads=[pada_tt])
                if pc == 7:
                    dve(lambda: nc.vector.scalar_tensor_tensor(out=Gc[:, l], in0=adacol[:, l, 8:16], scalar=1.0,
                                                               in1=ng_col[:, l], op0=ALU.add, op1=ALU.mult),
                        [ada_tt[l], adac_tt], [ada_tt[l]])
            else:
                g = pc - 8
                pg, pg_tt = alloc_R()
                for kc in range(8):
                    pe(lambda kc=kc: nc.tensor.matmul(pg[:, 0:256], lhsT=condbc[:, kc, :], rhs=wk[:, kc, :],
                                                      start=(kc == 0), stop=(kc == 7)),
                       wk_tts + [cond_tt], [pg_tt], signal=(kc == 7))
                dve(lambda: nc.vector.tensor_tensor(out=gate_bc[:, l, g * 256:(g + 1) * 256], in0=pg[:, 0:256],
                                                    in1=gate_bc[:, l, g * 256:(g + 1) * 256], op=ALU.add),
                    [gate_tt[l][g]], [gate_tt[l][g]], preads=[pg_tt])

        def first():
            for q in (range(8) if bufs is not None else range(1)):
                piece_dma(q)

        tasks = [first] + [(lambda pc=pc: task(pc)) for pc in range(12)]
        tasks_piece_dma[l] = piece_dma
        return tasks

    unit_ctr = [0]

    def p1a(t, l, blocks, halo_dma):
        n = len(blocks)
        u = unit_ctr[0] % 2
        unit_ctr[0] += 1
        srcs = []
        for i, b in enumerate(blocks):
            if halo_dma:
                if (t, b) in pre_issued:
                    pre_issued.discard((t, b))
                else:
                    dma(SP, xh_ds, xh[:], xin[t, b * 128:(b + 1) * 128, :], writes=[xh_tt])
                srcs.append((xh[:], [xh_tt]))
            else:
                srcs.append((xres[b - 1][:], xres_tt[b - 1]))
        for i in range(n):
            sap, stt = srcs[i]
            jk = i
            act(lambda i=i, sap=sap, jk=jk: nc.scalar.activation(out=xr[jk][:], in_=sap, func=AF.Square,
                                                                 accum_out=ssb[u][:, i:i + 1]),
                stt, [ssb_tt[u][i], xr_tt[jk]])
        act(lambda: nc.scalar.activation(out=msb[u][:, 0:n], in_=ssb[u][:, 0:n], func=AF.Ln, bias=eps_col[:, 0:1],
                                         scale=1.0 / D),
            ssb_tt[u][0:n] + [kc_tt], [msb_tt[u]])
        act(lambda: nc.scalar.activation(out=msb[u][:, 0:n], in_=msb[u][:, 0:n], func=AF.Exp, scale=-0.5),
            [msb_tt[u]], [msb_tt[u]])
        for i in range(n):
            sap, stt = srcs[i]
            dve(lambda i=i, sap=sap: nc.vector.tensor_scalar(out=xr[i][:], in0=sap, scalar1=msb[u][:, i:i + 1],
                                                             scalar2=None, op0=ALU.mult),
                list(stt) + [msb_tt[u]], [xr_tt[i]])

    def p1b(t, l, blocks, hb):
        n = len(blocks)
        H = hT[hb]
        Htt = hT_tt[hb]
        for c in range(8):
            pa, pa_tt = alloc_R()
            pab = pa.bitcast(BF16)
            for i in range(n):
                pe(lambda i=i, c=c: nc.tensor.transpose(pab[:, i * 128:(i + 1) * 128],
                                                        xr[i][:, c * 128:(c + 1) * 128], identb[:]),
                   [xr_tt[i], idb_tt], [pa_tt], signal=(i == n - 1))
            if c % 4 == 0:
                act(lambda c=c, pab=pab: nc.scalar.activation(out=H[:, c, 0:n * 128], in_=pab[:, 0:n * 128], func=AF.Identity,
                                                     bias=adacol[:, l, c:c + 1], scale=Gc[:, l, c:c + 1]),
                    [ada_tt[l]], [Htt[c]], preads=[pa_tt])
            else:
                dve(lambda c=c, pab=pab: nc.vector.tensor_scalar(out=H[:, c, 0:n * 128], in0=pab[:, 0:n * 128],
                                                        scalar1=Gc[:, l, c:c + 1], scalar2=adacol[:, l, c:c + 1],
                                                        op0=ALU.mult, op1=ALU.add),
                    [ada_tt[l]], [Htt[c]], preads=[pa_tt])
        pk, pk_tt = alloc_R()
        for kc in range(8):
            pe(lambda kc=kc: nc.tensor.matmul(pk[:, 0:n * 128], lhsT=Wkv[:, kc, 0:128], rhs=H[:, kc, 0:n * 128],
                                              start=(kc == 0), stop=(kc == 7)),
               [Wkv_tt, Htt[kc]], [pk_tt], signal=(kc == 7))
        act(lambda: nc.scalar.activation(out=ksq[:, 0:n * 128], in_=pk[:, 0:n * 128], func=AF.Square),
            [], [ksq_tt], preads=[pk_tt])
        pss, pss_tt = alloc_R()
        pe(lambda: nc.tensor.matmul(pss[:, 0:n * 128], lhsT=BD[:], rhs=ksq[:, 0:n * 128], start=True, stop=True),
           [ksq_tt, idb_tt], [pss_tt])
        act(lambda: nc.scalar.activation(out=tk[:, 0:n * 128], in_=pss[:, 0:n * 128], func=AF.Ln,
                                         bias=eps_col[:, 0:1], scale=1.0 / 64),
            [kc_tt], [tk_tt], preads=[pss_tt])
        act(lambda: nc.scalar.activation(out=tk[:, 0:n * 128], in_=tk[:, 0:n * 128], func=AF.Exp, scale=-0.5),
            [tk_tt], [tk_tt])
        b0 = blocks[0]
        for kv in range(2):
            for h in range(2):
                dve(lambda kv=kv, h=h: nc.vector.scalar_tensor_tensor(
                    out=Kpad[h][kv][h * 64:(h + 1) * 64, b0 * 128:(b0 + n) * 128],
                    in0=pk[kv * 64:(kv + 1) * 64, 0:n * 128], scalar=kg8[kv * 64:(kv + 1) * 64, l:l + 1],
                    in1=tk[kv * 64:(kv + 1) * 64, 0:n * 128], op0=ALU.mult, op1=ALU.mult),
                    [tk_tt, sm_tt], [Kpad_tt[h][kv][b] for b in blocks], preads=[pk_tt])
        for i, b in enumerate(blocks):
            pv, pv_tt = alloc_R()
            for kc in range(8):
                pe(lambda kc=kc, i=i: nc.tensor.matmul(pv[:, 0:128], lhsT=H[:, kc, i * 128:(i + 1) * 128],
                                                       rhs=Wkv[:, kc, 128:256], start=(kc == 0), stop=(kc == 7)),
                   [Wkv_tt, Htt[kc]], [pv_tt], signal=(kc == 7))
            vb0 = Vaug[:, b, 0, 0:64]
            vout = bass.AP(tensor=vb0.tensor, offset=vb0.offset, ap=[list(vb0.ap[0]), [192, 2], [1, 64]])
            dve(lambda vout=vout, pv=pv: nc.vector.tensor_copy(out=vout,
                                                               in_=pv[:, 0:128].rearrange("p (a e) -> p a e", a=2)),
                [], [Vaug_tt[b]], preads=[pv_tt])

    def stage_q(l, hb, jps=(0, 1)):
        H = hT[hb]
        Htt = hT_tt[hb]
        for jp in jps:
            pq, pq_tt = alloc_R()
            for jj in range(2):
                j = jp * 2 + jj
                for kc in range(8):
                    pe(lambda kc=kc, j=j, jj=jj, pq=pq: nc.tensor.matmul(
                        pq[:, jj * 256:(jj + 1) * 256], lhsT=Win[:, kc, j * 128:(j + 1) * 128],
                        rhs=H[:, kc, :], start=(kc == 0), stop=(kc == 7)),
                       [Wsec_tt["q"], Htt[kc]], [pq_tt], signal=(jj == 1 and kc == 7))
            act(lambda jp=jp, pq=pq: nc.scalar.activation(out=qsq[jp][:], in_=pq, func=AF.Square),
                [], [qsq_tt[jp]], preads=[pq_tt])
            dve(lambda jp=jp, pq=pq: nc.vector.tensor_scalar(out=qraw[jp][:], in0=pq, scalar1=qg_col[:, l:l + 1],
                                                             scalar2=None, op0=ALU.mult),
                [sm_tt], [qraw_tt[jp]], preads=[pq_tt])


    def p2(t, l, blocks, hb, early, filler, hooks, tick=None, q_pre=False, next_q=None, tail_early=None):
        H = hT[hb]
        Htt = hT_tt[hb]

        def hook(name):
            if hooks is not None and name in hooks:
                hooks[name]()
            if tick is not None and name in ("q", "ga", "gg", "u", "vgm"):
                tick()

        def proj_pair(sec, jp, evac):
            col0 = SEC[sec][0] + jp * 256
            pg, pg_tt = alloc_R()
            for jj in range(2):
                for kc in range(8):
                    pe(lambda kc=kc, jj=jj: nc.tensor.matmul(
                        pg[:, jj * 256:(jj + 1) * 256], lhsT=Win[:, kc, col0 + jj * 128:col0 + (jj + 1) * 128],
                        rhs=H[:, kc, :], start=(kc == 0), stop=(kc == 7)),
                       [Wsec_tt[sec], Htt[kc]], [pg_tt], signal=(jj == 1 and kc == 7))
            evac(pg, pg_tt)

        def stage_qnorm():
            for jp in range(2):
                pss, pss_tt = alloc_R()
                pe(lambda jp=jp, pss=pss: nc.tensor.matmul(pss, lhsT=BD[:], rhs=qsq[jp][:], start=True, stop=True),
                   [qsq_tt[jp], idb_tt], [pss_tt])
                act(lambda jp=jp, pss=pss: nc.scalar.activation(out=tq[jp][:], in_=pss, func=AF.Ln,
                                                                bias=eps_col[:, 0:1], scale=1.0 / 64),
                    [kc_tt], [tq_tt[jp]], preads=[pss_tt])
                act(lambda jp=jp: nc.scalar.activation(out=tq[jp][:], in_=tq[jp][:], func=AF.Exp, scale=-0.5),
                    [tq_tt[jp]], [tq_tt[jp]])
                for jj in range(2):
                    j = jp * 2 + jj
                    dve(lambda j=j, jj=jj, jp=jp: nc.vector.tensor_tensor(
                        out=QnT[:, :, j, :],
                        in0=qraw[jp][:, jj * 256:(jj + 1) * 256].rearrange("p (a b) -> p a b", a=2),
                        in1=tq[jp][:, jj * 256:(jj + 1) * 256].rearrange("p (a b) -> p a b", a=2), op=ALU.mult),
                        [tq_tt[jp], qraw_tt[jp]], [QnT_tt[j]])

        def stage_ga(jp):
            proj_pair("ga", jp, lambda pg, pg_tt: act(
                lambda: nc.scalar.activation(out=sg[:, 2 * jp:2 * jp + 2, :],
                                             in_=pg.rearrange("p (a b) -> p a b", a=2), func=AF.Silu),
                [], [sg_tt[jp]], preads=[pg_tt]))

        def stage_gg(jp):
            proj_pair("gg", jp, lambda pg, pg_tt: act(
                lambda: nc.scalar.activation(out=sgg[:, 2 * jp:2 * jp + 2, :],
                                             in_=pg.rearrange("p (a b) -> p a b", a=2), func=AF.Silu),
                [], [sgg_tt[jp]], preads=[pg_tt]))

        def stage_u(jp):
            proj_pair("u", jp, lambda pg, pg_tt: dve(
                lambda: nc.vector.tensor_tensor(out=ug[:, 2 * jp:2 * jp + 2, :],
                                                in0=pg.rearrange("p (a b) -> p a b", a=2),
                                                in1=sgg[:, 2 * jp:2 * jp + 2, :], op=ALU.mult),
                [sgg_tt[jp]], [ug_tt[jp]], preads=[pg_tt]))

        def stage_vgm(i):
            c0v = SEC["vgm"][0]
            pvg, pvg_tt = alloc_R()
            for kc in range(8):
                pe(lambda kc=kc: nc.tensor.matmul(pvg, lhsT=H[:, kc, i * 128:(i + 1) * 128],
                                                  rhs=Win[:, kc, c0v:c0v + 512], start=(kc == 0), stop=(kc == 7)),
                   [Wsec_tt["vgm"], Htt[kc]], [pvg_tt], signal=(kc == 7))
            act(lambda: nc.scalar.activation(out=vsq[i][:], in_=pvg, func=AF.Square),
                [], [vsq_tt[i]], preads=[pvg_tt])
            dve(lambda: nc.vector.tensor_reduce(out=ssv[i][:], in_=vsq[i][:].rearrange("p (g e) -> p g e", e=64),
                                                axis=AX.X, op=ALU.add),
                [vsq_tt[i]], [ssv_tt[i]])
            act(lambda: nc.scalar.activation(out=ssv[i][:], in_=ssv[i][:], func=AF.Ln, bias=eps_col[:, 0:1],
                                             scale=1.0 / 64),
                [ssv_tt[i], kc_tt], [ssv_tt[i]])
            act(lambda: nc.scalar.activation(out=ssv[i][:], in_=ssv[i][:], func=AF.Exp, scale=-0.5),
                [ssv_tt[i]], [ssv_tt[i]])
            vb = vnpad[i][:, 0, 0:64]
            out3 = bass.AP(tensor=vb.tensor, offset=vb.offset, ap=[list(vb.ap[0]), [256, 4], [192, 2], [1, 64]])
            rvb = ssv[i][:, 0:1]
            in1 = bass.AP(tensor=rvb.tensor, offset=rvb.offset, ap=[list(rvb.ap[0]), [2, 4], [1, 2], [0, 64]])
            dve(lambda: nc.vector.tensor_tensor(
                out=out3, in0=pvg.rearrange("p (pr two e) -> p pr two e", two=2, e=64), in1=in1, op=ALU.mult),
                [ssv_tt[i]], [vnpad_tt[i]], preads=[pvg_tt])

        pctr = [0]

        def stage_scores(i, kv):
            b = blocks[i]
            for r in range(3):
                kb = b + r - 1
                m = kv * 3 + r
                ps_, ps_tt = alloc_R()
                rhs = QnT[:, i, kv * 2:kv * 2 + 2, :]
                pe(lambda ps_=ps_, m=m: nc.tensor.matmul(ps_, lhsT=identb[:], rhs=Ec[:, m, :], start=True, stop=False),
                   [idb_tt, const_tt], [ps_tt], signal=False)
                for h in range(2):
                    pe(lambda h=h, kb=kb, ps_=ps_, rhs=rhs: nc.tensor.matmul(
                        ps_[:, h * 256:(h + 1) * 256], lhsT=Kpad[h][kv][:, kb * 128:(kb + 1) * 128], rhs=rhs,
                        start=False, stop=(h == 1)),
                       [Kpad_tt[h][kv][kb], QnT_tt[kv * 2], QnT_tt[kv * 2 + 1]], [ps_tt], signal=(h == 1))
                col = t * NB + kb
                act(lambda ps_=ps_, m=m, col=col: nc.scalar.activation(out=PT[m][:], in_=ps_, func=AF.Exp,
                                                                       bias=kmask[:, col:col + 1], scale=1.0),
                    [sm_tt], [PT_tt[m]], preads=[ps_tt])

        pvbank = {}

        def stage_pv(i, kv):
            b = blocks[i]
            ppv, ppv_tt = alloc_R()
            pvbank[kv] = (ppv, ppv_tt)
            for r in range(3):
                kb = b + r - 1
                m = kv * 3 + r
                pe(lambda r=r, kb=kb, m=m: nc.tensor.matmul(
                    ppv, lhsT=Vaug[:, kb, kv, :], rhs=PT[m][:], start=(r == 0), stop=(r == 2)),
                   [Vaug_tt[kb], PT_tt[m]], [ppv_tt], signal=(r == 2))
            rd = rden[i]
            dlo = 64 * (1 - kv)
            esb = es_sl[dlo:dlo + 64, l, kv * 4:kv * 4 + 4]
            dve(lambda: nc.vector.tensor_tensor(
                out=rd[dlo:dlo + 64, :].rearrange("p (a b) -> p a b", a=4),
                in0=ppv[dlo:dlo + 64, :].rearrange("p (a b) -> p a b", a=4), in1=_bc_last(esb, 128), op=ALU.add),
                [const_tt], [rden_tt[i]], preads=[ppv_tt])
            if kv == 0:
                return
            act(lambda: nc.scalar.activation(out=rd[:, :], in_=rd[:, :], func=AF.Ln), [rden_tt[i]], [rden_tt[i]])
            act(lambda: nc.scalar.activation(out=rd[:, :], in_=rd[:, :], func=AF.Exp, scale=-1.0),
                [rden_tt[i]], [rden_tt[i]])
            for kv2 in range(2):
                pp, pp_tt = pvbank[kv2]
                nlo = 64 * kv2
                dl2 = 64 * (1 - kv2)
                for h in range(2):
                    dve(lambda h=h, kv2=kv2, pp=pp, nlo=nlo, dl2=dl2: nc.vector.tensor_tensor(
                        out=ya[h * 64:(h + 1) * 64, kv2 * 2:kv2 * 2 + 2, :],
                        in0=pp[nlo:nlo + 64, h * 256:(h + 1) * 256].rearrange("p (a b) -> p a b", a=2),
                        in1=rd[dl2:dl2 + 64, h * 256:(h + 1) * 256].rearrange("p (a b) -> p a b", a=2), op=ALU.mult),
                        [rden_tt[i]], [ya_tt[kv2][h]], preads=[pp_tt])

        def stage_ya(i):
            dve(lambda: nc.vector.tensor_tensor(out=yT[i][:, 0:4, :], in0=ya[:],
                                                in1=sg[:, :, i * 128:(i + 1) * 128], op=ALU.mult),
                [ya_tt[0][0], ya_tt[0][1], ya_tt[1][0], ya_tt[1][1]] + sg_tt, [yTA_tt[i]])

        def stage_gmlp(i):
            psv, psv_tt = alloc_R()
            for pr in range(4):
                o = psv[:, pr * 128:(pr + 1) * 128]
                pe(lambda pr=pr, o=o: nc.tensor.matmul(o, lhsT=vnpad[i][:, 2 * pr, :], rhs=WsT[:, l, 2 * pr, :],
                                                       start=True, stop=False),
                   [vnpad_tt[i], wsT_tt], [psv_tt], signal=False)
                pe(lambda pr=pr, o=o: nc.tensor.matmul(o, lhsT=vnpad[i][:, 2 * pr + 1, :],
                                                       rhs=WsT[:, l, 2 * pr + 1, :], start=False, stop=False),
                   [vnpad_tt[i], wsT_tt], [psv_tt], signal=False)
                pe(lambda pr=pr, o=o: nc.tensor.matmul(o, lhsT=Lb[:], rhs=Rb[:, l, pr, :], start=False, stop=True),
                   [const_tt, rb_tt], [psv_tt], signal=(pr == 3))
            dve(lambda: nc.vector.tensor_tensor(
                out=yT[i][:, 4:8, :], in0=psv.rearrange("p (a b) -> p a b", a=4), in1=ug[:, :, i * 128:(i + 1) * 128],
                op=ALU.mult),
                ug_tt, [yTM_tt[i]], preads=[psv_tt])

        def stage_out(i, hfs=(0, 1)):
            b = blocks[i]
            for hf in hfs:
                po, po_tt = alloc_R()
                for c in range(8):
                    pe(lambda c=c, hf=hf, po=po: nc.tensor.matmul(
                        po, lhsT=yT[i][:, c, :], rhs=Wout[:, c, hf * 512:(hf + 1) * 512], start=(c == 0),
                        stop=(c == 7)),
                       [yTA_tt[i] if c < 4 else yTM_tt[i], Wout_tt[c // 4]], [po_tt], signal=(c == 7))
                xs = xres[b - 1][:, hf * 512:(hf + 1) * 512]
                gsl = gate_bc[:, l, hf * 512:(hf + 1) * 512]
                dve(lambda po=po, gsl=gsl: nc.vector.tensor_tensor(out=po, in0=po, in1=gsl, op=ALU.mult),
                    gate_tt[l][2 * hf:2 * hf + 2], [po_tt])
                dve(lambda po=po, xs=xs: nc.vector.tensor_tensor(out=xs, in0=po, in1=xs, op=ALU.add),
                    [xres_tt[b - 1][hf]], [xres_tt[b - 1][hf]], preads=[po_tt])
            if l == DEPTH - 1 and 1 in hfs:
                dma(SP, xres_ds[b - 1], out_d[t, (b - 2) * 128:(b - 1) * 128, :], xres[b - 1][:],
                    reads=xres_tt[b - 1])
                if t + 1 < NT:
                    dma(SP, xres_ds[b - 1], xres[b - 1][:], xin[t + 1, b * 128:(b + 1) * 128, :],
                        writes=xres_tt[b - 1])

        if early is not None:
            early()
        if not q_pre:
            stage_q(l, hb)
            stage_qnorm()
        hook("q")
        stage_ga(0)
        stage_ga(1)
        hook("ga")
        stage_gg(0)
        stage_scores(0, 0)
        stage_gg(1)
        hook("gg")
        stage_scores(0, 1)
        stage_u(0)
        if filler is not None:
            filler()
        hook("p1")
        stage_pv(0, 0)
        stage_u(1)
        stage_pv(0, 1)
        stage_ya(0)
        hook("u")
        stage_scores(1, 0)
        stage_vgm(0)
        stage_scores(1, 1)
        stage_vgm(1)
        hook("vgm")
        stage_pv(1, 0)
        stage_gmlp(0)
        stage_pv(1, 1)
        stage_ya(1)
        stage_gmlp(1)
        hook("pre_out")
        if tail_early is not None:
            tail_early()
        stage_out(0, (0,))
        if next_q is not None:
            next_q((0,))
        stage_out(0, (1,))
        if next_q is not None:
            next_q((1,))
        stage_out(1)
        if next_q is not None:
            stage_qnorm()
        hook("out")

    load_wkv(0)
    class _V:
        def __init__(self, ap):
            self.ap = ap

        def __getitem__(self, idx):
            return self.ap[idx]

    abufs = {}
    for pc in range(2, 8):
        xi = 4 + (pc - 2)
        abufs[pc] = (_V(xres[xi][:].bitcast(BF16).rearrange("p (k n) -> p k n", k=8)), xres_tt[xi],
                     DSem(sem(f"d_ast{pc}")))
    for pc in (10, 11):
        abufs[pc] = (abufs[pc - 4][0], abufs[pc - 4][1], DSem(sem(f"d_ast{pc}")))
    ada0 = ada_tasks(0, abufs)
    for tk_ in ada0[:9]:
        tk_()
    for k in ("q", "ga", "gg", "u", "vgm"):
        load_wsec(0, k)
    for h in range(2):
        for kv in range(2):
            pool(lambda h=h, kv=kv: nc.gpsimd.memset(Kpad[h][kv][:], 0.0), [], [x for x in Kpad_tt[h][kv]])
    pool(lambda: nc.gpsimd.memset(Vaug[:], 1.0), [], Vaug_tt)
    for i in range(2):
        pool(lambda i=i: nc.gpsimd.memset(vnpad[i][:], 0.0), [], [vnpad_tt[i]])
    tasks_piece_dma[0](10)
    tasks_piece_dma[0](11)
    load_wout(0)
    for l in range(DEPTH):
        dma(POOL, wsT_ds, WsT[:, l], wsT_d[l], writes=[wsT_tt])
    gate0 = ada0[9:]
    bg = list(ada_tasks(1))

    def drip(n=1):
        for _ in range(n):
            if bg:
                bg.pop(0)()

    def run_all(lst):
        while lst:
            lst.pop(0)()

    def layer_geom(l):
        lo, hi = l, NB - l
        sbs = [(b, b + 1) for b in range(lo + 1, hi - 1, 2)]
        return lo, hi, sbs

    def layer_head(t, l, part=0):
        lo, hi, sbs = layer_geom(l)
        halo = (l == 0)
        nxt_tile = (l == DEPTH - 1 and t + 1 < NT)
        if part in (0, 1):
            p1a(t, l, [lo], halo)
            if nxt_tile:
                dma(SP, xres_ds[lo - 1], xres[lo - 1][:], xin[t + 1, lo * 128:(lo + 1) * 128, :],
                    writes=xres_tt[lo - 1])
        if part == 1:
            return
        p1b(t, l, [lo], 1)
        p1a(t, l, list(sbs[0]), False)
        p1b(t, l, list(sbs[0]), 0)
        if first_sb_of(t, l) == 1:
            p1a(t, l, list(sbs[1]), False)
            p1b(t, l, list(sbs[1]), 1)
            for j in range(2):
                dve(lambda j=j: nc.vector.tensor_copy(out=xres[j][:], in_=stash[j]), [wst_tt[j]], xres_tt[j])
        s0h = first_sb_of(t, l)
        nsb_h = len(sbs)
        if s0h + 1 < nsb_h:
            p1a(t, l, list(sbs[s0h + 1]), False)
        else:
            p1a(t, l, [hi - 1], halo)

    def first_sb_of(t, l):
        return 1 if (l == 0 and t > 0) else 0

    stash = [wst[j][:].rearrange("p a b -> p (a b)").bitcast(F32) for j in range(2)]
    layer_head(0, 0)
    late_setup()
    for b in range(5, NB - 3):
        dma(SP, xres_ds[b - 1], xres[b - 1][:], xin[0, b * 128:(b + 1) * 128, :], writes=xres_tt[b - 1])

    def late_x_loads():
        for b in range(NB - 3, NB - 1):
            dma(SP, xres_ds[b - 1], xres[b - 1][:], xin[0, b * 128:(b + 1) * 128, :], writes=xres_tt[b - 1])
    q_pre = [False]
    for t in range(NT):
        for l in range(DEPTH):
            lo, hi, sbs = layer_geom(l)
            nsb = len(sbs)
            halo = (l == 0)
            last_layer = (t == NT - 1 and l == DEPTH - 1)
            nl_ = (l + 1) % DEPTH
            nt_ = t if l + 1 < DEPTH else t + 1
            nxt_tile = (l == DEPTH - 1 and t + 1 < NT)
            s0 = first_sb_of(t, l)
            for s in range(s0, nsb):
                if s + 1 < nsb:
                    nblocks, nhb, nhalo = list(sbs[s + 1]), (s + 1) % 2, False
                else:
                    nblocks, nhb, nhalo = [hi - 1], nsb % 2, halo

                def mk_early(s2):
                    if s2 + 1 < nsb:
                        nb2, nh2 = list(sbs[s2 + 1]), False
                    else:
                        nb2, nh2 = [hi - 1], halo

                    def early():
                        p1a(t, l, nb2, nh2)
                        if nxt_tile and s2 == nsb - 1:
                            bb = hi - 1
                            dma(SP, xres_ds[bb - 1], xres[bb - 1][:], xin[t + 1, bb * 128:(bb + 1) * 128, :],
                                writes=xres_tt[bb - 1])
                    return early

                early = None
                tail_early = mk_early(s + 1) if s + 1 < nsb else None

                filler = (lambda nblocks=nblocks, nhb=nhb: p1b(t, l, nblocks, nhb))
                hooks = {}
                if s == nsb - 1 and not last_layer:
                    hooks = {k: (lambda k=k: load_wsec(nl_, k)) for k in ("q", "ga", "gg", "u", "vgm")}

                    def after_vgm(nl_=nl_, nt_=nt_):
                        load_wsec(nl_, "vgm")
                        drip(100)
                        layer_head(nt_, nl_, part=2)

                    def after_u(nl_=nl_, nt_=nt_):
                        load_wsec(nl_, "u")
                        drip(100)
                        layer_head(nt_, nl_, part=1)

                    hooks["p1"] = (lambda nl_=nl_: load_wkv(nl_))
                    hooks["u"] = after_u
                    hooks["vgm"] = after_vgm
                    hooks["out"] = (lambda: load_wout(nl_))
                first_sb = (t == 0 and l == 0 and s == 0)
                if first_sb:
                    hooks["gg"] = gate0.pop(0)
                    hooks["u"] = gate0.pop(0)
                    hooks["vgm"] = gate0.pop(0)
                    g_last = gate0.pop(0)
                    hooks["pre_out"] = (lambda g_last=g_last: (g_last(), late_x_loads()))
                tick = drip if (t == 0 and l == 0 and s > 0) else None
                if s + 1 < nsb:
                    next_q = (lambda jps, s=s: stage_q(l, (s + 1) % 2, jps))
                elif not last_layer:
                    next_q = (lambda jps, nl_=nl_, hb_=first_sb_of(nt_, nl_) % 2: stage_q(nl_, hb_, jps))
                else:
                    next_q = None
                p2(t, l, sbs[s], s % 2, early, filler, hooks, tick=tick, q_pre=q_pre[0], next_q=next_q,
                   tail_early=tail_early)
                q_pre[0] = next_q is not None
            if l == 0 and t + 1 < NT:
                for j in range(2):
                    bb = NB - 3 + j
                    dve(lambda j=j, bb=bb: nc.vector.tensor_copy(out=stash[j], in_=xres[bb - 1][:]),
                        xres_tt[bb - 1], [wst_tt[j]])
    for ds in xres_ds:
        SP.h.wait_ge(ds.sem, ds.n)
    return nc, es


def _host_constants():
    slopes = 2.0 ** (-8.0 * np.arange(1, 9, dtype=np.float64) / 8)
    kl = np.arange(128)[:, None]
    ql = np.arange(128)[None, :]
    E = np.zeros((128, 6, 4, 128), np.float64)
    for kv in range(2):
        for r in range(3):
            if r == 0:
                dist = 128 + ql - kl
            elif r == 1:
                dist = np.abs(ql - kl)
            else:
                dist = 128 + kl - ql
            valid = dist <= 128
            for s in range(4):
                c = kv * 2 + (s % 2)
                head = 2 * c + (s // 2)
                E[:, kv * 3 + r, s, :] = np.where(valid, -slopes[head] * dist, MASKV)
    E = E.reshape(128, 6, 512).astype(np.float32).astype(ml_dtypes.bfloat16)
    bd = np.zeros((128, 128), np.float32)
    bd[0:64, 0:64] = 1.0
    bd[64:128, 64:128] = 1.0
    lb = np.zeros((128, 128), np.float32)
    lb[0, 0:64] = 1.0
    lb[64, 0:64] = 1.0
    lb[32, 64:128] = 1.0
    lb[96, 64:128] = 1.0
    return E, bd.astype(ml_dtypes.bfloat16), lb.astype(ml_dtypes.bfloat16), np.eye(128, dtype=np.float32)


_SLOT_HEADS = [0, 2, 1, 3, 4, 6, 5, 7]


def make_in_maps(x, c, w_ada, b_ada, norm_gain, w_in, q_gain, k_gain, sink, w_s, b_s, w_out):
    f = lambda a: np.ascontiguousarray(np.asarray(a), dtype=np.float32)
    x, c, w_ada, b_ada, norm_gain, w_in = f(x), f(c), f(w_ada), f(b_ada), f(norm_gain), f(w_in)
    q_gain, k_gain, sink, w_s, b_s, w_out = f(q_gain), f(k_gain), f(sink), f(w_s), f(b_s), f(w_out)
    E, bd, lb, ident = _host_constants()
    shared = {
        "w_ada": w_ada,
        "bada_col": np.ascontiguousarray(b_ada[:, 0:2048].reshape(DEPTH, 16, 128).transpose(0, 2, 1)),
        "bada_gate": np.ascontiguousarray(b_ada[:, 2048:3072].reshape(DEPTH, 1, D)),
        "ng_col": np.ascontiguousarray(norm_gain.reshape(DEPTH, 8, 128).transpose(0, 2, 1)),
        "w_in": w_in,
        "qg_col": np.ascontiguousarray(np.concatenate([q_gain, q_gain], axis=1).T),
        "kg_col": np.ascontiguousarray(np.concatenate([k_gain, k_gain], axis=1).T),
        "sinkp": np.ascontiguousarray(sink[:, _SLOT_HEADS].reshape(DEPTH, 1, 8)),
        "wsT": np.ascontiguousarray(w_s.transpose(0, 3, 1, 2)),
        "bs": b_s,
        "w_out": w_out,
        "ident": ident,
        "bd": bd,
        "identb": ident.astype(ml_dtypes.bfloat16),
        "lb": lb,
        "econst": E,
    }
    nblk_seq = SEQ // 128
    in_maps = []
    for core in range(NCORES):
        b = core // 4
        ci = core % 4
        xin = np.zeros((NT, NB * 128, D), np.float32)
        km = np.zeros((128, NT * NB), np.float32)
        for t in range(NT):
            g0 = ci * 16 + t * OWN - 2
            for lb_ in range(NB):
                g = g0 + lb_
                if 0 <= g < nblk_seq:
                    xin[t, lb_ * 128:(lb_ + 1) * 128] = x[b, g * 128:(g + 1) * 128]
                else:
                    km[:, t * NB + lb_] = MASKV
        m = dict(shared)
        m["xin"] = xin
        m["kmask"] = km
        m["ccol"] = np.ascontiguousarray(c[b].reshape(8, 128).T)
        in_maps.append(m)
    return in_maps


_CACHE = {}


def kernel(x, c, w_ada, b_ada, norm_gain, w_in, q_gain, k_gain, sink, w_s, b_s, w_out):
    in_maps = make_in_maps(x, c, w_ada, b_ada, norm_gain, w_in, q_gain, k_gain, sink, w_s, b_s, w_out)
    if "nc" not in _CACHE:
        _CACHE["nc"] = build_program()
    nc, _es = _CACHE["nc"]
    res = run_bass_kernel_spmd(nc, in_maps, core_ids=list(range(NCORES)))
    out = np.zeros((2, SEQ, D), np.float32)
    for core in range(NCORES):
        b = core // 4
        ci = core % 4
        o = res.results[core]["out"]
        for t in range(NT):
            g = ci * 16 + t * OWN
            out[b, g * 128:(g + OWN) * 128] = o[t]
    return out
```
